# Optimizing a Trainium2 kernel written in Bass

```python
import math
import jax, jax.numpy as jnp
from jax import lax
import numpy as np

D_MODEL = 1024
BATCH = 2
SEQ = 8192
DEPTH = 4
DEC_BATCH = 128
DEC_SEQ = 4
PAST_LEN = 8192
PAGE_SIZE = 128

N_EVEN = (DEPTH + 1) // 2
N_ODD = DEPTH // 2
NORM_EPS = 1e-6
NEG_INF = -1e30

SWA_HEADS = 8
SWA_KV_HEADS = 2
SWA_HEAD_DIM = 64
SWA_GROUP = SWA_HEADS // SWA_KV_HEADS
SWA_WIDTH = SWA_HEADS * SWA_HEAD_DIM
SWA_KV_WIDTH = SWA_KV_HEADS * SWA_HEAD_DIM
WINDOW = 128
SWA_BLOCK = WINDOW
REL_BUCKETS = 32
REL_MAX_DIST = 128

GMLP_GROUPS = 4
GMLP_CHUNK = 128
GMLP_WIDTH = D_MODEL // 2
GMLP_GROUP_DIM = GMLP_WIDTH // GMLP_GROUPS

EVEN_SIZES = (SWA_WIDTH, SWA_KV_WIDTH, SWA_KV_WIDTH, SWA_WIDTH, GMLP_WIDTH, GMLP_WIDTH, GMLP_WIDTH)
EVEN_IN = 3 * SWA_WIDTH // 1 - SWA_WIDTH + 2 * SWA_KV_WIDTH + 3 * GMLP_WIDTH
EVEN_MIX = SWA_WIDTH + GMLP_WIDTH

RET_HEADS = D_MODEL // 256
RET_KEY_DIM = 256
RET_VALUE_DIM = 2 * RET_KEY_DIM
RET_QK_WIDTH = RET_HEADS * RET_KEY_DIM
RET_V_WIDTH = RET_HEADS * RET_VALUE_DIM
RET_CHUNK = 128
ODD_SIZES = (RET_QK_WIDTH, RET_QK_WIDTH, RET_V_WIDTH, RET_V_WIDTH)
ODD_IN = 2 * RET_QK_WIDTH + 2 * RET_V_WIDTH

kernel_name = "hybrid_swa_gmlp_retention_step"


def _split(z, sizes):
    out, start = [], 0
    for s in sizes:
        out.append(z[..., start:start + s])
        start += s
    return out


def rmsnorm(x, g):
    x32 = x.astype(jnp.float32)
    y = x32 * lax.rsqrt(jnp.mean(x32 * x32, axis=-1, keepdims=True) + NORM_EPS)
    return (y * g.astype(jnp.float32)).astype(x.dtype)


def t5_bucket(dist):
    n = jnp.maximum(dist, 0)
    max_exact = REL_BUCKETS // 2
    nf = jnp.maximum(n, 1).astype(jnp.float32)
    large = max_exact + (jnp.log(nf / max_exact) / math.log(REL_MAX_DIST / max_exact)
                         * (REL_BUCKETS - max_exact)).astype(jnp.int32)
    large = jnp.minimum(large, REL_BUCKETS - 1)
    return jnp.where(n < max_exact, n, large)


def rel_bias(dist, table):
    b = table[t5_bucket(dist)].astype(jnp.float32)
    b = jnp.moveaxis(b, -1, 0)
    return b.reshape(SWA_KV_HEADS, SWA_GROUP, *dist.shape)


def sink_softmax(scores, sink):
    sink = sink.astype(jnp.float32)
    m = jnp.maximum(scores.max(axis=-1, keepdims=True), sink)
    p = jnp.exp(scores - m)
    return p / (p.sum(axis=-1, keepdims=True) + jnp.exp(sink - m))


def swa_prompt(q, k, v, sinks, table):
    B, S = q.shape[:2]
    nb = S // SWA_BLOCK
    qb = q.reshape(B, nb, SWA_BLOCK, SWA_KV_HEADS, SWA_GROUP, SWA_HEAD_DIM)
    kb = k.reshape(B, nb, SWA_BLOCK, SWA_KV_HEADS, SWA_HEAD_DIM)
    vb = v.reshape(B, nb, SWA_BLOCK, SWA_KV_HEADS, SWA_HEAD_DIM)
    shift = lambda a: jnp.concatenate([jnp.zeros_like(a[:, :1]), a[:, :-1]], axis=1)
    kk = jnp.concatenate([shift(kb), kb], axis=2)
    vv = jnp.concatenate([shift(vb), vb], axis=2)
    scores = jnp.einsum('bnqkgd,bnskd->bnkgqs', qb, kk).astype(jnp.float32) * (SWA_HEAD_DIM ** -0.5)
    qi = jnp.arange(SWA_BLOCK)[:, None]
    sj = jnp.arange(2 * SWA_BLOCK)[None, :]
    dist = qi + SWA_BLOCK - sj
    kpos = jnp.arange(nb)[:, None, None] * SWA_BLOCK - SWA_BLOCK + sj[None]
    valid = (dist >= 0) & (dist < WINDOW) & (kpos >= 0)
    bias = rel_bias(dist, table)
    scores = jnp.where(valid[None, :, None, None], scores + bias[None, None], NEG_INF)
    sink = sinks.reshape(SWA_KV_HEADS, SWA_GROUP)[None, None, :, :, None, None]
    probs = sink_softmax(scores, sink)
    out = jnp.einsum('bnkgqs,bnskd->bnqkgd', probs.astype(v.dtype), vv)
    return out.reshape(B, S, SWA_WIDTH)


def swa_sample(q, k, v, cache_k, cache_v, sinks, table):
    DB, T = q.shape[:2]
    kk = jnp.concatenate([cache_k.astype(k.dtype), k], axis=1)
    vv = jnp.concatenate([cache_v.astype(v.dtype), v], axis=1)
    scores = jnp.einsum('btkgd,bskd->bkgts', q, kk).astype(jnp.float32) * (SWA_HEAD_DIM ** -0.5)
    dist = jnp.arange(T)[:, None] + WINDOW - jnp.arange(WINDOW + T)[None, :]
    valid = (dist >= 0) & (dist < WINDOW)
    bias = rel_bias(dist, table)
    scores = jnp.where(valid, scores + bias[None], NEG_INF)
    sink = sinks.reshape(SWA_KV_HEADS, SWA_GROUP)[None, :, :, None, None]
    probs = sink_softmax(scores, sink)
    out = jnp.einsum('bkgts,bskd->btkgd', probs.astype(v.dtype), vv)
    return out.reshape(DB, T, SWA_WIDTH), kk[:, -WINDOW:], vv[:, -WINDOW:]


def gmlp_spatial(u, vb, ws, bs, ln_gain):
    B, T = u.shape[:2]
    L = min(T, GMLP_CHUNK)
    nc = T // L
    v32 = vb.astype(jnp.float32)
    mu = jnp.mean(v32, axis=-1, keepdims=True)
    var = jnp.mean(jnp.square(v32 - mu), axis=-1, keepdims=True)
    vn = ((v32 - mu) * lax.rsqrt(var + NORM_EPS) * ln_gain.astype(jnp.float32)).astype(u.dtype)
    vc = vn.reshape(B, nc, L, GMLP_GROUPS, GMLP_GROUP_DIM)
    wm = jnp.tril(ws[:, :L, :L]).astype(vc.dtype)
    s = jnp.einsum('gpq,bnqgc->bnpgc', wm, vc) + bs[:, :L].T[None, None, :, :, None]
    return u * s.reshape(B, T, GMLP_WIDTH), vn


def even_layer(x, g_norm, w_in, w_out, sinks, table, ws, bs, ln_gain, cache_k, cache_v):
    B, T = x.shape[:2]
    z = rmsnorm(x, g_norm) @ w_in
    q, k, v, ga, u, vb, gb = _split(z, EVEN_SIZES)
    q = q.reshape(B, T, SWA_KV_HEADS, SWA_GROUP, SWA_HEAD_DIM)
    k = k.reshape(B, T, SWA_KV_HEADS, SWA_HEAD_DIM)
    v = v.reshape(B, T, SWA_KV_HEADS, SWA_HEAD_DIM)
    if cache_k is None:
        attn = swa_prompt(q, k, v, sinks, table)
        new_k, new_v = k[:, -WINDOW:], v[:, -WINDOW:]
    else:
        attn, new_k, new_v = swa_sample(q, k, v, cache_k, cache_v, sinks, table)
    sg, vn = gmlp_spatial(u, vb, ws, bs, ln_gain)
    mix = jnp.concatenate([jax.nn.silu(ga) * attn.astype(x.dtype), jax.nn.silu(gb) * sg], axis=-1)
    return x + mix @ w_out, new_k, new_v, vn


def xpos_rotate(x, pos):
    angle = 1.0 / (10000.0 ** jnp.linspace(0.0, 1.0, RET_KEY_DIM // 2, dtype=jnp.float32))
    ang = pos.astype(jnp.float32)[:, None] * angle[None, :]
    sin = jnp.sin(ang)[:, None, :]
    cos = jnp.cos(ang)[:, None, :]
    x32 = x.astype(jnp.float32)
    x0, x1 = x32[..., 0::2], x32[..., 1::2]
    return jnp.stack([x0 * cos - x1 * sin, x1 * cos + x0 * sin], axis=-1).reshape(x.shape)


def retention_scan(q, k, v, S0, chunk):
    B, T, H, _ = q.shape
    nc = T // chunk
    lg = jnp.log(1.0 - 2.0 ** (-5.0 - jnp.arange(RET_HEADS, dtype=jnp.float32)))
    idx = jnp.arange(chunk, dtype=jnp.float32)
    diff = idx[:, None] - idx[None, :]
    decay = jnp.where(diff >= 0, jnp.exp(lg[:, None, None] * jnp.maximum(diff, 0.0)), 0.0)
    q_dec = jnp.exp(lg[None, :] * (idx[:, None] + 1.0))
    k_dec = jnp.exp(lg[None, :] * (chunk - 1.0 - idx)[:, None])
    c_dec = jnp.exp(lg * chunk)

    def step(S, inp):
        qc, kc, vc = inp
        att = jnp.einsum('bihd,bjhd->bhij', qc, kc) * decay
        intra = jnp.einsum('bhij,bjhe->bihe', att, vc)
        cross = jnp.einsum('bihd,bhde->bihe', qc, S) * q_dec[None, :, :, None]
        S = c_dec[None, :, None, None] * S + jnp.einsum('bjhd,bjhe->bhde', kc * k_dec[None, :, :, None], vc)
        return S, intra + cross

    blk = lambda a: a.reshape(B, nc, chunk, H, a.shape[-1]).swapaxes(0, 1)
    S, o = lax.scan(step, S0, (blk(q), blk(k), blk(v)))
    return o.swapaxes(0, 1).reshape(B, T, H, RET_VALUE_DIM), S


def odd_layer(x, g_norm, w_in, w_out, S0, pos0):
    B, T = x.shape[:2]
    z = rmsnorm(x, g_norm) @ w_in
    q, k, v, g = _split(z, ODD_SIZES)
    pos = pos0 + jnp.arange(T, dtype=jnp.int32)
    q = xpos_rotate(q.reshape(B, T, RET_HEADS, RET_KEY_DIM), pos)
    k = xpos_rotate(k.reshape(B, T, RET_HEADS, RET_KEY_DIM), pos) * (RET_KEY_DIM ** -0.5)
    v = v.reshape(B, T, RET_HEADS, RET_VALUE_DIM).astype(jnp.float32)
    o, S = retention_scan(q, k, v, S0.astype(jnp.float32), min(T, RET_CHUNK))
    mu = jnp.mean(o, axis=-1, keepdims=True)
    var = jnp.mean(jnp.square(o - mu), axis=-1, keepdims=True)
    o = ((o - mu) * lax.rsqrt(var + NORM_EPS)).reshape(B, T, RET_V_WIDTH).astype(x.dtype)
    return x + (jax.nn.silu(g) * o) @ w_out, S


def setup_inputs(seed: int = 0) -> dict:
    key = jax.random.key(seed)
    ks = jax.random.split(key, 18)
    f32 = jnp.float32
    nrm = lambda k, shape, s: jax.random.normal(k, shape, f32) * s
    resid = (2.0 * DEPTH) ** -0.5
    return {
        'x_prompt': nrm(ks[0], (BATCH, SEQ, D_MODEL), 1.0),
        'x_sample': nrm(ks[1], (DEC_BATCH, DEC_SEQ, D_MODEL), 1.0),
        'cache_swa_k': nrm(ks[2], (N_EVEN, DEC_BATCH, WINDOW, SWA_KV_HEADS, SWA_HEAD_DIM), 1.0),
        'cache_swa_v': nrm(ks[3], (N_EVEN, DEC_BATCH, WINDOW, SWA_KV_HEADS, SWA_HEAD_DIM), 1.0),
        'state_ret': nrm(ks[4], (N_ODD, DEC_BATCH, RET_HEADS, RET_KEY_DIM, RET_VALUE_DIM), 0.3),
        'norm_gain': 1.0 + nrm(ks[5], (DEPTH, D_MODEL), 0.05),
        'final_norm_gain': 1.0 + nrm(ks[6], (D_MODEL,), 0.05),
        'rel_bias_table': nrm(ks[7], (REL_BUCKETS, SWA_HEADS), 0.5),
        'even_w_in': nrm(ks[8], (N_EVEN, D_MODEL, EVEN_IN), D_MODEL ** -0.5),
        'even_w_out': nrm(ks[9], (N_EVEN, EVEN_MIX, D_MODEL), EVEN_MIX ** -0.5 * resid),
        'swa_sinks': nrm(ks[10], (N_EVEN, SWA_HEADS), 1.0),
        'gmlp_ws': nrm(ks[11], (N_EVEN, GMLP_GROUPS, GMLP_CHUNK, GMLP_CHUNK), GMLP_CHUNK ** -0.5),
        'gmlp_bs': 1.0 + nrm(ks[12], (N_EVEN, GMLP_GROUPS, GMLP_CHUNK), 0.1),
        'gmlp_ln_gain': 1.0 + nrm(ks[13], (N_EVEN, GMLP_WIDTH), 0.05),
        'odd_w_in': nrm(ks[14], (N_ODD, D_MODEL, ODD_IN), D_MODEL ** -0.5),
        'odd_w_out': nrm(ks[15], (N_ODD, RET_V_WIDTH, D_MODEL), RET_V_WIDTH ** -0.5 * resid),
    }


def reference(x_prompt, x_sample, cache_swa_k, cache_swa_v, state_ret, norm_gain, final_norm_gain,
              rel_bias_table, even_w_in, even_w_out, swa_sinks, gmlp_ws, gmlp_bs, gmlp_ln_gain,
              odd_w_in, odd_w_out):
    yp, ys = x_prompt, x_sample
    kp_l, vp_l, ks_l, vs_l, sp_l, ss_l, gv_l = [], [], [], [], [], [], []
    for layer in range(DEPTH):
        if layer % 2 == 0:
            e = layer // 2
            yp, kp, vp, _ = even_layer(yp, norm_gain[layer], even_w_in[e], even_w_out[e], swa_sinks[e],
                                       rel_bias_table, gmlp_ws[e], gmlp_bs[e], gmlp_ln_gain[e], None, None)
            ys, kn, vn_, gv = even_layer(ys, norm_gain[layer], even_w_in[e], even_w_out[e], swa_sinks[e],
                                         rel_bias_table, gmlp_ws[e], gmlp_bs[e], gmlp_ln_gain[e],
                                         cache_swa_k[e], cache_swa_v[e])
            kp_l.append(kp); vp_l.append(vp); ks_l.append(kn); vs_l.append(vn_); gv_l.append(gv)
        else:
            o = layer // 2
            S0 = jnp.zeros((yp.shape[0], RET_HEADS, RET_KEY_DIM, RET_VALUE_DIM), jnp.float32)
            yp, sp = odd_layer(yp, norm_gain[layer], odd_w_in[o], odd_w_out[o], S0, 0)
            ys, ss = odd_layer(ys, norm_gain[layer], odd_w_in[o], odd_w_out[o], state_ret[o], PAST_LEN)
            sp_l.append(sp); ss_l.append(ss)
    y_prompt = rmsnorm(yp, final_norm_gain)
    y_sample = rmsnorm(ys, final_norm_gain)
    return (y_prompt, y_sample, jnp.stack(kp_l), jnp.stack(vp_l), jnp.stack(ks_l), jnp.stack(vs_l),
            jnp.stack(sp_l), jnp.stack(ss_l), jnp.stack(gv_l))
```

```python
import math
from contextlib import ExitStack

import numpy as np

import concourse.bass as bass
import concourse.mybir as mybir
from concourse.bass_utils import run_bass_kernel_spmd

F32 = mybir.dt.float32
BF16 = mybir.dt.bfloat16
U8 = mybir.dt.uint8
AF = mybir.ActivationFunctionType
ALU = mybir.AluOpType

D = 1024
NCH = 16
NEG = -30000.0
EPS = 1e-6
GAM = [1.0 - 2.0 ** (-5.0 - h) for h in range(4)]
E_IN = 2816
O_IN = 6144
COMPUTE = ("pe", "act", "dve", "pool")


class Op:
    __slots__ = ("idx", "eng", "fn", "dma", "cc", "deps", "has_dep", "sem", "val")

    def __init__(self, idx, eng, fn, dma, cc):
        self.idx, self.eng, self.fn, self.dma, self.cc = idx, eng, fn, dma, cc
        self.deps = {}
        self.has_dep = False
        self.sem = None
        self.val = None


class Prog:
    def __init__(self, nc):
        self.nc = nc
        self.ops = []
        self.last_write = {}
        self.readers = {}
        self.frontier = set()
        self.pending = {}
        self.n_dma_sems = {"sp": 4, "pool": 4, "act": 2}

    def op(self, eng, fn, reads=(), writes=(), dma=False, cc=False, nb=False):
        o = Op(len(self.ops), eng, fn, dma, cc)
        pr = [r for r in reads if isinstance(r, str) and r.startswith("ps") and r not in writes]
        if pr:
            writes = list(writes) + pr
        for r in reads:
            w = self.last_write.get(r)
            if w is not None:
                o.deps[w] = True
        for r in writes:
            w = self.last_write.get(r)
            if w is not None and w not in o.deps:
                o.deps[w] = False
            latest = {}
            for rd in self.readers.get(r, ()):
                ro = self.ops[rd]
                if ro.dma or ro.cc:
                    if rd not in o.deps:
                        o.deps[rd] = False
                elif latest.get(ro.eng, -1) < rd:
                    latest[ro.eng] = rd
            for rd in latest.values():
                if rd not in o.deps:
                    o.deps[rd] = False
        pb = self.pending.pop(eng, None)
        if pb is not None:
            for p in pb:
                if p not in o.deps:
                    o.deps[p] = False
        for r in reads:
            self.readers.setdefault(r, []).append(o.idx)
        for r in writes:
            self.last_write[r] = o.idx
            self.readers[r] = []
        for p in o.deps:
            self.frontier.discard(p)
        if not nb:
            self.frontier.add(o.idx)
        self.ops.append(o)
        return o

    def barrier(self):
        fr = sorted(self.frontier)
        for e in ("pe", "act", "dve", "pool", "sp"):
            prev = self.pending.get(e)
            self.pending[e] = fr if prev is None else sorted(set(prev) | set(fr))

    def pe(self, fn, r=(), w=()):
        return self.op("pe", fn, r, w)

    def act(self, fn, r=(), w=()):
        return self.op("act", fn, r, w)

    def dve(self, fn, r=(), w=()):
        return self.op("dve", fn, r, w)

    def pool(self, fn, r=(), w=()):
        return self.op("pool", fn, r, w)

    def dma(self, q, out, in_, r=(), w=(), nb=False, **kw):
        return self.op(q, lambda e: e.dma_start(out=out, in_=in_, **kw), r, w, dma=True, nb=nb)

    def emit(self, stack):
        nc, ops = self.nc, self.ops

        def skip(po, o, is_raw):
            return po.eng == o.eng == "pe" and not is_raw

        for o in ops:
            for p, is_raw in o.deps.items():
                if not skip(ops[p], o, is_raw):
                    ops[p].has_dep = True
        sems = {e: stack.enter_context(nc.semaphore("sem_" + e)) for e in COMPUTE}
        dsems = {q: [stack.enter_context(nc.semaphore(f"ds_{q}{i}")) for i in range(n)]
                 for q, n in self.n_dma_sems.items()}
        cc_sem = stack.enter_context(nc.semaphore("cc_sem"))
        counts = {e: 0 for e in COMPUTE}
        dk = {q: 0 for q in dsems}
        ccn = 0
        for o in ops:
            if o.cc:
                ccn += 1
                o.sem, o.val = cc_sem, ccn
            elif o.dma:
                k, n = dk[o.eng], len(dsems[o.eng])
                o.sem, o.val = dsems[o.eng][k % n], 16 * (k // n + 1)
                dk[o.eng] = k + 1
            elif o.has_dep:
                counts[o.eng] += 1
                o.sem, o.val = sems[o.eng], counts[o.eng]
        streams = {e: [] for e in ("pe", "act", "dve", "pool", "sp")}
        for o in ops:
            streams[o.eng].append(o)
        self.stats = {e: len(s) for e, s in streams.items()}
        self.stats["counts"] = dict(counts)
        self.stats["dma"] = dict(dk)

        def run(name, eng):
            waited = {}
            issued = []
            nw = 0
            for o in streams[name]:
                need = {}
                for p, is_raw in o.deps.items():
                    po = ops[p]
                    if skip(po, o, is_raw):
                        continue
                    if need.get(id(po.sem), (None, 0))[1] < po.val:
                        need[id(po.sem)] = (po.sem, po.val)
                if o.dma and not o.cc and o.val > 16:
                    if need.get(id(o.sem), (None, 0))[1] < o.val - 16:
                        need[id(o.sem)] = (o.sem, o.val - 16)
                for sid, (s, v) in need.items():
                    if waited.get(sid, 0) >= v:
                        continue
                    eng.wait_ge(s, v)
                    waited[sid] = v
                    nw += 1
                ins = o.fn(eng)
                if o.cc:
                    ins.then_inc(o.sem)
                    issued.append(o)
                elif o.dma:
                    ins.then_inc(o.sem, 16)
                    issued.append(o)
                elif o.sem is not None:
                    ins.then_inc(o.sem, 1)
            fin = {}
            for o in issued:
                if fin.get(id(o.sem), (None, 0))[1] < o.val:
                    fin[id(o.sem)] = (o.sem, o.val)
            for sid, (s, v) in fin.items():
                if waited.get(sid, 0) < v:
                    eng.wait_ge(s, v)
            self.stats[name + "_w"] = nw

        block = stack.enter_context(nc.Block())

        @block.tensor
        def _(e):
            run("pe", e)

        @block.scalar
        def _(e):
            run("act", e)

        @block.vector
        def _(e):
            run("dve", e)

        @block.gpsimd
        def _(e):
            run("pool", e)

        @block.sync
        def _(e):
            run("sp", e)


class Buf:
    __slots__ = ("ap", "key")

    def __init__(self, ap, key):
        self.ap, self.key = ap, key

    def __getitem__(self, idx):
        return self.ap[idx]


class Arena:
    def __init__(self, nc, stack, name, nbytes):
        self.t = stack.enter_context(nc.sbuf_tensor(name, [128, nbytes], U8))
        self.nbytes = nbytes
        self.off = 0
        self.gen = 0
        self.name = name

    def reset(self):
        self.off = 0
        self.gen += 1

    def alloc(self, name, shape, dt):
        esz = 4 if dt == F32 else 2
        free = int(np.prod(shape[1:]))
        nb = (free * esz + 31) // 32 * 32
        assert self.off + nb <= self.nbytes, (self.name, name, self.off, nb, self.nbytes)
        ap = self.t[0:shape[0], self.off:self.off + free * esz].bitcast(dt)
        if len(shape) == 3:
            ap = ap.rearrange("p (a b) -> p a b", b=shape[2])
        elif len(shape) == 4:
            ap = ap.rearrange("p (a b c) -> p a b c", b=shape[2], c=shape[3])
        self.off += nb
        return Buf(ap, f"{self.name}{self.gen}:{name}")

    def rot(self, name, shape, dt, n):
        return [self.alloc(f"{name}{i}", shape, dt) for i in range(n)]


def bcast_mid(ap2d, n):
    return ap2d.unsqueeze(1).to_broadcast([ap2d.shape[0], n, ap2d.shape[1]])


def bcast_last(ap2d, n):
    return ap2d.unsqueeze(2).to_broadcast([ap2d.shape[0], ap2d.shape[1], n])


class Builder:
    def __init__(self, cfg):
        self.cfg = cfg
        self.nc = bass.Bass("TRN2", target_bir_lowering=False)
        self.stack = ExitStack()
        self.P = Prog(self.nc)
        self.psum_i = 0
        self.rr = {}
        self.reserved = set()

    def dram_in(self, name, shape, dt=F32):
        t = self.nc.dram_tensor(name, list(shape), dt, kind="ExternalInput")
        return t

    def dram_out(self, name, shape, dt=F32):
        return self.nc.dram_tensor(name, list(shape), dt, kind="ExternalOutput")

    def dram_scr(self, name, shape, dt):
        return self.nc.dram_tensor(name, list(shape), dt)

    def sb(self, name, shape, dt):
        t = self.stack.enter_context(self.nc.sbuf_tensor("sb_" + name, list(shape), dt))
        return Buf(t[:], name)

    def bank(self):
        while True:
            b = self.banks[self.psum_i % 8]
            self.psum_i += 1
            if b.key not in self.reserved:
                return b

    def nxt(self, lst, name):
        i = self.rr.get(name, 0)
        self.rr[name] = i + 1
        return lst[i % len(lst)]

    def declare(self):
        nc = self.nc
        di = {}
        di["xp"] = self.dram_in("xp", [NCH * 128, D])
        di["xh"] = self.dram_in("xh", [128, D])
        di["xs"] = self.dram_in("xs", [64, D])
        di["ck"] = self.dram_in("ck", [2, 16, 128, 128])
        di["cv"] = self.dram_in("cv", [2, 16, 128, 128])
        big = self.cfg.get("sample", True)
        di["st"] = self.dram_in("st", [2, 16, 4, 256, 512] if big else [1, 1, 1, 2, 512])
        di["we_in"] = self.dram_in("we_in", [2, D, E_IN])
        di["we_out"] = self.dram_in("we_out", [2, D, D])
        di["wo_in"] = self.dram_in("wo_in", [2, D, O_IN])
        di["wo_out"] = self.dram_in("wo_out", [2, 2048, D])
        di["ngain"] = self.dram_in("ngain", [5, D])
        di["table"] = self.dram_in("table", [32, 8])
        di["sinks"] = self.dram_in("sinks", [2, 8])
        di["gws"] = self.dram_in("gws", [2, 4, 128, 128])
        di["gbs"] = self.dram_in("gbs", [2, 4, 128])
        di["glg"] = self.dram_in("glg", [2, 512])
        di["ident"] = self.dram_in("ident", [128, 128])
        di["oh"] = self.dram_in("oh", [33, 384])
        di["rotp"] = self.dram_in("rotp", [2, 128, NCH * 128])
        di["rots"] = self.dram_in("rots", [2, 128, 64])
        di["kdec"] = self.dram_in("kdec", [128, 4, 128])
        di["kdecs"] = self.dram_in("kdecs", [128, 4, 64])
        di["epsp"] = self.dram_in("epsp", [128, 4])
        di["epsps"] = self.dram_in("epsps", [64, 4])
        di["caus"] = self.dram_in("caus", [128, 128])
        di["bdm"] = self.dram_in("bdm", [64, 64])
        di["coef"] = self.dram_in("coef", [128, 32])
        di["hsel"] = self.dram_in("hsel", [128, 8])
        di["hneg"] = self.dram_in("hneg", [128, 1])
        di["rowm"] = self.dram_in("rowm", [64, 16])
        di["selm"] = self.dram_in("selm", [4, 64])
        self.di = di
        do = {}
        do["yp"] = self.dram_out("yp", [NCH * 128, D])
        do["ys"] = self.dram_out("ys", [64, D])
        do["kvp"] = self.dram_out("kvp", [2, 128, 256])
        do["ksn"] = self.dram_out("ksn", [2, 16, 128, 128])
        do["vsn"] = self.dram_out("vsn", [2, 16, 128, 128])
        do["retp"] = self.dram_out("retp", [2, 4, 256, 512])
        do["rets"] = self.dram_out("rets", [2, 16, 4, 256, 512] if big else [1, 1, 1, 2, 512])
        do["gv"] = self.dram_out("gv", [2, 64, 512])
        self.do = do
        sc = {}
        sc["G"] = self.dram_scr("G", [8, 128, 384], F32)
        sc["xd"] = self.dram_scr("xd", [NCH, 128, D], F32)
        sc["xnT"] = self.dram_scr("sxnT", [NCH, 128, D], BF16)
        sc["kT"] = self.dram_scr("skT", [NCH, 128, D], BF16)
        sc["kt"] = self.dram_scr("skt", [NCH, 128, D], BF16)
        sc["v"] = self.dram_scr("sv", [NCH, 128, 2048], BF16)
        sc["mix"] = self.dram_scr("smix", [NCH, 128, 2048], BF16)
        sc["cc_src"] = self.dram_scr("cc_src", [8 * 128, 512], F32)
        sc["cc_dst"] = self.dram_scr("cc_dst", [8 * 8 * 128, 512], F32)
        sc["hx_src"] = self.dram_scr("hx_src", [128, 256], F32)
        sc["hx_dst"] = self.dram_scr("hx_dst", [8 * 128, 256], F32)
        sc["kv4"] = self.dram_scr("kv4", [64, 256], F32)
        sc["s_xnT"] = self.dram_scr("s_xnT", [128, 512], BF16)
        sc["s_kT"] = self.dram_scr("s_kT", [128, 512], BF16)
        sc["s_kt"] = self.dram_scr("s_kt", [64, D], BF16)
        sc["s_v"] = self.dram_scr("s_v", [64, 2048], BF16)
        self.sc = sc

    def alloc_static(self):
        nc, st = self.nc, self.stack
        self.banks = []
        for i in range(8):
            t = st.enter_context(nc.psum_tensor(f"ps{i}", [128, 512], F32))
            self.banks.append(Buf(t[:], f"ps{i}"))
        self.W1 = self.sb("W1", [128, 8, 3072], BF16)
        self.W2 = self.sb("W2", [128, 16, 1024], BF16)
        self.W3 = self.sb("W3", [128, 8, E_IN], BF16)
        self.identb = self.sb("identb", [128, 128], BF16)
        self.identf = self.sb("identf", [128, 128], F32)
        self.epsc = self.sb("epsc", [128, 1], F32)
        self.onec = self.sb("onec", [128, 1], F32)
        self.caus = self.sb("caus", [128, 128], F32)
        self.gbc = self.sb("gbc", [128, D], F32)
        self.xs = self.sb("xs_res", [64, D], F32)
        self.junk = self.sb("junk", [128, D], BF16)
        self.stat = [self.sb(f"stat{i}", [128, 4], F32) for i in range(3)]
        self.xn = [self.sb(f"xn{i}", [128, D], BF16) for i in range(2)]
        self.U = Arena(nc, st, "U", self.cfg.get("ubytes", 64 * 1024))

    def load_w(self, dst, kc_off, src_ap_rows, col0, ncols, dcol0, nkc=8, nb=True):
        for kc in range(nkc):
            self.P.dma("pool", dst[:, kc_off + kc, dcol0:dcol0 + ncols],
                       src_ap_rows[kc * 128:(kc + 1) * 128, col0:col0 + ncols],
                       w=[(dst.key, kc_off + kc)], nb=nb)

    def wkeys(self, buf, kcs=range(8)):
        return [(buf.key, k) for k in kcs]

    def rmsnorm_T(self, src_ap, src_key, npart, xnT):
        P = self.P
        st = self.nxt(self.stat, "stat")
        xn = self.nxt(self.xn, "xn")
        junk, epsc, gbc, identb = self.junk, self.epsc, self.gbc, self.identb
        n = npart
        P.act(lambda e: e.activation(out=junk[0:n, :], in_=src_ap, func=AF.Square, accum_out=st[0:n, 0:1]),
              r=[src_key], w=[junk.key, st.key])
        P.act(lambda e: e.activation(out=st[0:n, 1:2], in_=st[0:n, 0:1], func=AF.Ln, scale=1.0 / D,
                                     bias=epsc[0:n, 0:1]), r=[st.key, epsc.key], w=[st.key])
        P.act(lambda e: e.activation(out=st[0:n, 2:3], in_=st[0:n, 1:2], func=AF.Exp, scale=-0.5),
              r=[st.key], w=[st.key])
        P.dve(lambda e: e.scalar_tensor_tensor(out=xn[0:n, :], in0=src_ap, scalar=st[0:n, 2:3], in1=gbc[0:n, :],
                                               op0=ALU.mult, op1=ALU.mult),
              r=[src_key, st.key, gbc.key], w=[xn.key])
        bk = self.bank()
        bkb = bk.ap.bitcast(BF16)
        for k in range(8):
            P.pe(lambda e, k=k: e.transpose(out=bkb[:, k * n:(k + 1) * n], in_=xn[0:n, k * 128:(k + 1) * 128],
                                            identity=identb[0:n, 0:n]),
                 r=[xn.key, identb.key], w=[bk.key])
        P.act(lambda e: e.copy(out=xnT[:, :, 0:n], in_=bkb[:, 0:8 * n].rearrange("p (k t) -> p k t", t=n)),
              r=[bk.key], w=[xnT.key])

    def proj_tok(self, xnT, n, W, wcol0, ncols, bk, dcol0=0, kcs=8):
        for kc in range(kcs):
            self.P.pe(lambda e, kc=kc: e.matmul(out=bk[0:n, dcol0:dcol0 + ncols], lhsT=xnT[:, kc, 0:n],
                                                rhs=W[:, kc, wcol0:wcol0 + ncols], start=(kc == 0),
                                                stop=(kc == kcs - 1)),
                      r=[xnT.key, (W.key, kc)], w=[bk.key])

    def proj_feat(self, xnT, n, W, wcol0, bk, dcol0):
        for kc in range(8):
            self.P.pe(lambda e, kc=kc: e.matmul(out=bk[:, dcol0:dcol0 + n], lhsT=W[:, kc, wcol0:wcol0 + 128],
                                                rhs=xnT[:, kc, 0:n], start=(kc == 0), stop=(kc == 7)),
                      r=[xnT.key, (W.key, kc)], w=[bk.key])

    def sigmoid_from(self, src_ap, src_key, tmp, n):
        P, onec = self.P, self.onec
        P.act(lambda e: e.activation(out=tmp.ap, in_=src_ap, func=AF.Exp, scale=-1.0), r=[src_key], w=[tmp.key])
        P.act(lambda e: e.activation(out=tmp.ap, in_=tmp.ap, func=AF.Ln, bias=onec[0:n, 0:1]),
              r=[tmp.key, onec.key], w=[tmp.key])
        P.act(lambda e: e.activation(out=tmp.ap, in_=tmp.ap, func=AF.Exp, scale=-1.0), r=[tmp.key], w=[tmp.key])

    def setup(self):
        P, di, sc, U = self.P, self.di, self.sc, self.U
        identb, identf, epsc, onec, caus = self.identb, self.identf, self.epsc, self.onec, self.caus
        P.dma("pool", identb.ap, di["ident"].ap(), w=[identb.key])
        P.dma("sp", identf.ap, di["ident"].ap(), w=[identf.key])
        P.dma("sp", caus.ap, di["caus"].ap(), w=[caus.key])
        P.dma("sp", self.xs.ap, di["xs"].ap(), w=[self.xs.key])
        P.dve(lambda e: e.memset(epsc.ap, EPS), w=[epsc.key])
        P.dve(lambda e: e.memset(onec.ap, 1.0), w=[onec.key])
        U.reset()
        tab = U.alloc("tab", [33, 8], F32)
        oh = U.alloc("oh", [33, 384], F32)
        ones33 = U.alloc("ones33", [33, 128], F32)
        lhs = U.rot("lhs", [33, 128], F32, 2)
        Fbc = U.alloc("Fbc", [128, 8, 384], F32)
        P.dve(lambda e: e.memset(tab[32:33, :], NEG), w=[tab.key])
        P.dma("sp", tab[0:32, :], di["table"].ap(), w=[tab.key])
        P.dma("sp", oh.ap, di["oh"].ap(), w=[oh.key])
        P.dve(lambda e: e.memset(ones33.ap, 1.0), w=[ones33.key])
        for h in range(8):
            l = lhs[h % 2]
            bk = self.bank()
            P.dve(lambda e, l=l, h=h: e.tensor_scalar(out=l.ap, in0=ones33.ap, scalar1=tab[:, h:h + 1], scalar2=None,
                                                      op0=ALU.mult), r=[ones33.key, tab.key], w=[l.key])
            P.pe(lambda e, l=l, bk=bk: e.matmul(out=bk[:, 0:384], lhsT=l.ap, rhs=oh.ap, start=True, stop=True),
                 r=[l.key, oh.key], w=[bk.key])
            P.act(lambda e, bk=bk, h=h: e.copy(out=Fbc[:, h, :], in_=bk[:, 0:384]), r=[bk.key], w=[Fbc.key])
        P.dma("sp", sc["G"].ap().rearrange("h p m -> p h m"), Fbc.ap, r=[Fbc.key], w=["G"])
        P.barrier()

    def toeplitz(self, base, nrow, ncol):
        G = self.sc["G"]
        return bass.AP(tensor=G, offset=base, ap=[[383, nrow], [128 * 384, 8], [1, ncol]])

    def even_consts(self, e, layer, prompt):
        P, di, U = self.P, self.di, self.U
        gbc, caus, identf = self.gbc, self.caus, self.identf
        c = {}
        P.dma("sp", gbc.ap, di["ngain"].ap()[layer:layer + 1, :].partition_broadcast(128), w=[gbc.key])
        c["esink"] = esink = U.alloc("esink", [128, 8], F32)
        c["wmT"] = wmT = U.alloc("wmT", [128, 4, 128], BF16)
        c["bsb"] = bsb = U.alloc("bsb", [128, 4], F32)
        c["lng"] = lng = U.alloc("lng", [128, 512], F32)
        wraw = U.alloc("wraw", [128, 4, 128], F32)
        P.dma("sp", esink.ap, di["sinks"].ap()[e:e + 1, :].partition_broadcast(128), w=[esink.key])
        P.act(lambda e_: e_.activation(out=esink.ap, in_=esink.ap, func=AF.Exp), r=[esink.key], w=[esink.key])
        P.dma("sp", wraw.ap, di["gws"].ap()[e].rearrange("g p q -> p g q"), w=[wraw.key])
        bk = self.bank()
        for g in range(4):
            P.pe(lambda e_, g=g: e_.transpose(out=bk[:, g * 128:(g + 1) * 128], in_=wraw[:, g, :], identity=identf.ap),
                 r=[wraw.key, identf.key], w=[bk.key])
        P.dve(lambda e_: e_.tensor_tensor(out=wmT.ap, in0=bk.ap.rearrange("p (g q) -> p g q", q=128),
                                          in1=bcast_mid(caus.ap, 4), op=ALU.mult),
              r=[bk.key, caus.key], w=[wmT.key])
        P.dma("sp", bsb.ap, di["gbs"].ap()[e].rearrange("g p -> p g"), w=[bsb.key], allow_slow_non_contiguous=True)
        P.dma("sp", lng.ap, di["glg"].ap()[e:e + 1, :].partition_broadcast(128), w=[lng.key])
        return c

    def even_prompt(self, e, layer):
        P, di, do, sc, U = self.P, self.di, self.do, self.sc, self.U
        W3, W2, identb = self.W3, self.W2, self.identb
        U.reset()
        c = self.even_consts(e, layer, True)
        esink, wmT, bsb, lng = c["esink"], c["wmT"], c["bsb"], c["lng"]
        biasT = U.alloc("biasT", [128, 3, 8, 128], F32)
        hneg = U.alloc("hneg", [128, 1], F32)
        for var, base in ((0, 128), (1, 256)):
            P.dma("sp", biasT[:, var, :, :], self.toeplitz(base, 128, 128), r=["G"], w=[(biasT.key, var)])
        P.dma("sp", hneg.ap, di["hneg"].ap(), w=[hneg.key])
        P.dve(lambda e_: e_.tensor_scalar(out=biasT[:, 2, :, :], in0=biasT[:, 1, :, :], scalar1=hneg[:, 0:1],
                                          scalar2=None, op0=ALU.add),
              r=[(biasT.key, 1), hneg.key], w=[(biasT.key, 2)])
        if self.cfg.get("cut") == 2:
            P.barrier()
            return
        wo = di["wo_in"].ap()[e]
        self.load_w(self.W1, 0, wo, 1024, 3072, 0)
        xt = U.rot("xt", [128, D], F32, 2)
        xnT = U.rot("xnT", [128, 8, 128], BF16, 2)
        qT = U.rot("qT", [128, 4, 128], BF16, 2)
        kb = U.rot("kb", [128, 128], BF16, 2)
        kT = U.rot("kT", [128, 128], BF16, 3)
        Va = U.rot("Va", [128, 2, 65], BF16, 3)
        kvo = U.alloc("kvo", [128, 256], F32)
        scr = U.rot("sc", [128, 512], F32, 2)
        expT = U.alloc("expT", [128, 2, 2, 512], BF16)
        den = U.alloc("den", [128, 8], F32)
        rden = U.alloc("rden", [128, 8], F32)
        attn = U.alloc("attn", [128, 512], F32)
        tA = U.rot("tA", [128, 512], F32, 2)
        tB = U.rot("tB", [128, 512], F32, 2)
        bst = U.alloc("bst", [128, 6], F32)
        mv = U.alloc("mv", [128, 4], F32)
        vnb = U.alloc("vnb", [128, 512], BF16)
        mix = U.alloc("mix", [128, D], BF16)
        mixT = U.alloc("mixT", [128, 8, 128], BF16)
        for v in Va:
            P.dve(lambda e_, v=v: e_.memset(v[:, :, 64:65], 1.0), w=[v.key])

        def kv_block(x_ap, x_key, slot, last):
            xT = self.nxt(xnT, "e_xnT")
            self.rmsnorm_T(x_ap, x_key, 128, xT)
            bk = self.bank()
            self.proj_tok(xT, 128, W3, 512, 256, bk)
            k_b = self.nxt(kb, "e_kb")
            P.dve(lambda e_: e_.tensor_copy(out=k_b.ap, in_=bk[:, 0:128]), r=[bk.key], w=[k_b.key])
            va = Va[slot]
            P.dve(lambda e_: e_.tensor_copy(out=va[:, :, 0:64], in_=bk[:, 128:256].rearrange("p (g d) -> p g d", d=64)),
                  r=[bk.key], w=[va.key])
            if last and not self.cfg.get("nolast"):
                P.act(lambda e_: e_.copy(out=kvo.ap, in_=bk[:, 0:256]), r=[bk.key], w=[kvo.key])
                P.dma("sp", do["kvp"].ap()[e], kvo.ap, r=[kvo.key])
            bt = self.bank()
            btb = bt.ap.bitcast(BF16)
            P.pe(lambda e_: e_.transpose(out=btb[:, 0:128], in_=k_b.ap, identity=identb.ap),
                 r=[k_b.key, identb.key], w=[bt.key])
            kt = kT[slot]
            P.act(lambda e_: e_.copy(out=kt.ap, in_=btb[:, 0:128]), r=[bt.key], w=[kt.key])
            return xT

        if layer == 0:
            xh = self.nxt(xt, "e_xt")
            P.dma("sp", xh.ap, di["xh"].ap(), w=[xh.key])
            kv_block(xh.ap, xh.key, 0, False)
        else:
            self.halo_exchange(e, layer, xt, xnT, kb, kT[0], Va[0], kvo)

        def load(n):
            x = self.nxt(xt, "e_xt")
            src = di["xp"].ap()[n * 128:(n + 1) * 128, :] if layer == 0 else sc["xd"].ap()[n]
            P.dma("sp", x.ap, src, r=[] if layer == 0 else [("xd", n)], w=[x.key])
            return x

        def block(n, x):
            cur, prev = (n + 1) % 3, n % 3
            xT = kv_block(x.ap, x.key, cur, n == NCH - 1)
            if self.cfg.get("cut") == 3:
                return
            bq = self.bank()
            for j in range(4):
                self.proj_feat(xT, 128, W3, j * 128, bq, j * 128)
            q_T = self.nxt(qT, "e_qT")
            P.act(lambda e_, bq=bq, q_T=q_T: e_.activation(out=q_T.ap.rearrange("p j t -> p (j t)"), in_=bq.ap,
                                                           func=AF.Copy, scale=0.125), r=[bq.key], w=[q_T.key])
            for bi, slot in enumerate((prev, cur)):
                var = (2 if n == 0 else 1) if bi == 0 else 0
                for g2 in range(2):
                    bs_ = self.bank()
                    P.pe(lambda e_, bs_=bs_, slot=slot, g2=g2, q_T=q_T: e_.matmul(
                        out=bs_.ap, lhsT=kT[slot][g2 * 64:(g2 + 1) * 64, :],
                        rhs=q_T[g2 * 64:(g2 + 1) * 64, :, :], start=True, stop=True),
                        r=[kT[slot].key, q_T.key], w=[bs_.key])
                    s_ = self.nxt(scr, "e_sc")
                    P.dve(lambda e_, bs_=bs_, s_=s_, var=var, g2=g2: e_.tensor_tensor(
                        out=s_.ap, in0=bs_.ap, in1=biasT[:, var, g2 * 4:(g2 + 1) * 4, :].rearrange("p j q -> p (j q)"),
                        op=ALU.add), r=[bs_.key, (biasT.key, var)], w=[s_.key])
                    P.act(lambda e_, s_=s_, bi=bi, g2=g2: e_.activation(out=expT[:, bi, g2, :], in_=s_.ap, func=AF.Exp),
                          r=[s_.key], w=[(expT.key, bi, g2)])
            bp = [self.bank(), self.bank()]
            for h in range(8):
                g2, j = h // 4, h % 4
                for bi, slot in enumerate((prev, cur)):
                    P.pe(lambda e_, h=h, g2=g2, j=j, bi=bi, slot=slot: e_.matmul(
                        out=bp[g2][:, j * 65:(j + 1) * 65], lhsT=expT[:, bi, g2, j * 128:(j + 1) * 128],
                        rhs=Va[slot][:, g2, :], start=(bi == 0), stop=(bi == 1)),
                        r=[(expT.key, bi, g2), Va[slot].key], w=[bp[g2].key])
            for g2 in range(2):
                P.dve(lambda e_, g2=g2: e_.tensor_tensor(
                    out=den[:, g2 * 4:(g2 + 1) * 4], in0=bp[g2][:, 0:260].rearrange("p (j d) -> p j d", d=65)[:, :, 64],
                    in1=esink[:, g2 * 4:(g2 + 1) * 4], op=ALU.add), r=[bp[g2].key, esink.key], w=[den.key])
            P.dve(lambda e_: e_.reciprocal(out=rden.ap, in_=den.ap), r=[den.key], w=[rden.key])
            for g2 in range(2):
                P.dve(lambda e_, g2=g2: e_.tensor_tensor(
                    out=attn[:, g2 * 256:(g2 + 1) * 256].rearrange("p (j d) -> p j d", d=64),
                    in0=bp[g2][:, 0:260].rearrange("p (j d) -> p j d", d=65)[:, :, 0:64],
                    in1=bcast_last(rden[:, g2 * 4:(g2 + 1) * 4], 64), op=ALU.mult),
                    r=[bp[g2].key, rden.key], w=[attn.key])
            if self.cfg.get("cut") == 4:
                return
            self.even_gates(xT, 128, attn, lng, wmT, bsb, tA, tB, bst, mv, vnb, mix, None)
            if self.cfg.get("cut") == 5:
                return
            self.out_proj(mix, 8, mixT, W2, x, 128)
            P.dma("sp", sc["xd"].ap()[n], x.ap, r=[x.key], w=[("xd", n)])

        nb_ = self.cfg.get("nblk", NCH)
        cur_x = load(0)
        for n in range(nb_):
            nxt_x = load(n + 1) if n + 1 < nb_ else None
            block(n, cur_x)
            cur_x = nxt_x
        P.barrier()

    def even_gates(self, xT, n, attn, lng, wmT, bsb, tA, tB, bst, mv, vnb, mix, gv_out):
        P, W3, epsc = self.P, self.W3, self.epsc
        bga = self.bank()
        self.proj_tok(xT, n, W3, 768, 512, bga)
        a = self.nxt(tA, "e_tA")
        a_n = Buf(a[0:n, :], a.key)
        self.sigmoid_from(bga[0:n, :], bga.key, a_n, n)
        P.dve(lambda e_: e_.tensor_tensor(out=a_n.ap, in0=bga[0:n, :], in1=a_n.ap, op=ALU.mult),
              r=[bga.key, a.key], w=[a.key])
        P.dve(lambda e_: e_.tensor_tensor(out=mix[0:n, 0:512], in0=a_n.ap, in1=attn[0:n, :], op=ALU.mult),
              r=[a.key, attn.key], w=[(mix.key, 0)])
        bvb = self.bank()
        self.proj_tok(xT, n, W3, 1792, 512, bvb)
        P.dve(lambda e_: e_.bn_stats(out=bst[0:n, :], in_=bvb[0:n, :]), r=[bvb.key], w=[bst.key])
        P.dve(lambda e_: e_.bn_aggr(out=mv[0:n, 0:2], in_=bst[0:n, :]), r=[bst.key], w=[mv.key])
        P.act(lambda e_: e_.activation(out=mv[0:n, 2:3], in_=mv[0:n, 1:2], func=AF.Ln, bias=epsc[0:n, 0:1]),
              r=[mv.key, epsc.key], w=[mv.key])
        P.act(lambda e_: e_.activation(out=mv[0:n, 3:4], in_=mv[0:n, 2:3], func=AF.Exp, scale=-0.5),
              r=[mv.key], w=[mv.key])
        b = self.nxt(tB, "e_tB")
        b_n = Buf(b[0:n, :], b.key)
        P.dve(lambda e_: e_.tensor_scalar(out=b_n.ap, in0=bvb[0:n, :], scalar1=mv[0:n, 0:1], scalar2=mv[0:n, 3:4],
                                          op0=ALU.subtract, op1=ALU.mult), r=[bvb.key, mv.key], w=[b.key])
        if gv_out is not None:
            gvt, gv_dst = gv_out
            P.dve(lambda e_: e_.tensor_tensor(out=gvt[0:n, :], in0=b_n.ap, in1=lng[0:n, :], op=ALU.mult),
                  r=[b.key, lng.key], w=[gvt.key])
            P.dma("sp", gv_dst, gvt[0:n, :], r=[gvt.key])
        P.dve(lambda e_: e_.tensor_tensor(out=vnb[0:n, :], in0=b_n.ap, in1=lng[0:n, :], op=ALU.mult),
              r=[b.key, lng.key], w=[vnb.key])
        bs_ = self.bank()
        for g in range(4):
            P.pe(lambda e_, g=g: e_.matmul(out=bs_[0:n, g * 128:(g + 1) * 128], lhsT=wmT[0:n, g, 0:n],
                                           rhs=vnb[0:n, g * 128:(g + 1) * 128], start=True, stop=True),
                 r=[wmT.key, vnb.key], w=[bs_.key])
        P.dve(lambda e_: e_.tensor_tensor(out=b_n.ap.rearrange("p (g c) -> p g c", c=128),
                                          in0=bs_[0:n, :].rearrange("p (g c) -> p g c", c=128),
                                          in1=bcast_last(bsb[0:n, :], 128), op=ALU.add),
              r=[bs_.key, bsb.key], w=[b.key])
        bu = self.bank()
        self.proj_tok(xT, n, W3, 1280, 512, bu)
        P.dve(lambda e_: e_.tensor_tensor(out=b_n.ap, in0=bu[0:n, :], in1=b_n.ap, op=ALU.mult),
              r=[bu.key, b.key], w=[b.key])
        bgb = self.bank()
        self.proj_tok(xT, n, W3, 2304, 512, bgb)
        a2 = self.nxt(tA, "e_tA")
        a2_n = Buf(a2[0:n, :], a2.key)
        self.sigmoid_from(bgb[0:n, :], bgb.key, a2_n, n)
        P.dve(lambda e_: e_.tensor_tensor(out=a2_n.ap, in0=bgb[0:n, :], in1=a2_n.ap, op=ALU.mult),
              r=[bgb.key, a2.key], w=[a2.key])
        P.dve(lambda e_: e_.tensor_tensor(out=mix[0:n, 512:1024], in0=a2_n.ap, in1=b_n.ap, op=ALU.mult),
              r=[a2.key, b.key], w=[(mix.key, 1)])

    def out_proj(self, mix, nk, mixT, W, x, n, wkc0=0, mixkeys=None):
        P, identb = self.P, self.identb
        if mixkeys is None:
            mixkeys = [(mix.key, 0), (mix.key, 1)] if nk == 8 else [mix.key]
        for half in range(nk // 8):
            bt = self.bank()
            btb = bt.ap.bitcast(BF16)
            for k in range(8):
                kk = half * 8 + k
                P.pe(lambda e_, k=k, kk=kk, btb=btb: e_.transpose(out=btb[:, k * n:(k + 1) * n],
                                                                  in_=mix[0:n, kk * 128:(kk + 1) * 128],
                                                                  identity=identb[0:n, 0:n]),
                     r=mixkeys + [identb.key], w=[bt.key])
            P.act(lambda e_, half=half, btb=btb: e_.copy(out=mixT[:, half * 8:(half + 1) * 8, 0:n],
                                                         in_=btb[:, 0:8 * n].rearrange("p (k t) -> p k t", t=n)),
                  r=[bt.key], w=[(mixT.key, half)])
        for nh in range(2):
            by = self.bank()
            for kc in range(nk):
                P.pe(lambda e_, kc=kc, nh=nh, by=by: e_.matmul(out=by[0:n, :], lhsT=mixT[:, kc, 0:n],
                                                               rhs=W[:, wkc0 + kc, nh * 512:(nh + 1) * 512],
                                                               start=(kc == 0), stop=(kc == nk - 1)),
                     r=[(mixT.key, kc // 8), (W.key, wkc0 + kc)], w=[by.key])
            P.dve(lambda e_, nh=nh, by=by: e_.tensor_tensor(out=x[0:n, nh * 512:(nh + 1) * 512], in0=by[0:n, :],
                                                            in1=x[0:n, nh * 512:(nh + 1) * 512], op=ALU.add),
                  r=[by.key, x.key], w=[x.key])

    def halo_exchange(self, e, layer, xt, xnT, kb, kT0, Va0, kvo):
        P, U, sc, di, W3, identb = self.P, self.U, self.sc, self.di, self.W3, self.identb
        x = self.nxt(xt, "e_xt")
        P.dma("sp", x.ap, sc["xd"].ap()[NCH - 1], r=[("xd", NCH - 1)], w=[x.key])
        xT = self.nxt(xnT, "e_xnT")
        self.rmsnorm_T(x.ap, x.key, 128, xT)
        bk = self.bank()
        self.proj_tok(xT, 128, W3, 512, 256, bk)
        P.act(lambda e_: e_.copy(out=kvo.ap, in_=bk[:, 0:256]), r=[bk.key], w=[kvo.key])
        P.dma("sp", sc["hx_src"].ap(), kvo.ap, r=[kvo.key], w=["hx_src"])
        P.op("pool", lambda e_: e_.collective_compute(
            "AllGather", ALU.bypass, replica_groups=[list(range(8))],
            ins=[sc["hx_src"].ap().opt()], outs=[sc["hx_dst"].ap().opt()]),
            reads=["hx_src"], writes=["hx_dst"], cc=True)
        hx = U.alloc("hx", [128, 3, 256], F32)
        hsel = U.alloc("hsel", [128, 8], F32)
        hk = U.alloc("hk", [128, 256], F32)
        P.dma("sp", hsel.ap, di["hsel"].ap(), w=[hsel.key])
        for half in range(2):
            P.dma("sp", hx.ap, sc["hx_dst"].ap()[half * 512:half * 512 + 384, :].rearrange("(j p) c -> p j c", p=128),
                  r=["hx_dst"], w=[hx.key])
            for jj in range(3):
                j = half * 4 + jj
                if j == 0:
                    P.dve(lambda e_: e_.tensor_scalar(out=hk.ap, in0=hx[:, 0, :], scalar1=hsel[:, 0:1], scalar2=None, op0=ALU.mult),
                          r=[hx.key, hsel.key], w=[hk.key])
                else:
                    P.dve(lambda e_, j=j, jj=jj: e_.scalar_tensor_tensor(out=hk.ap, in0=hx[:, jj, :], scalar=hsel[:, j:j + 1], in1=hk.ap,
                                                                         op0=ALU.mult, op1=ALU.add), r=[hx.key, hsel.key, hk.key], w=[hk.key])
        k_b = self.nxt(kb, "e_kb")
        P.dve(lambda e_: e_.tensor_copy(out=k_b.ap, in_=hk[:, 0:128]), r=[hk.key], w=[k_b.key])
        P.dve(lambda e_: e_.tensor_copy(out=Va0[:, :, 0:64], in_=hk[:, 128:256].rearrange("p (g d) -> p g d", d=64)),
              r=[hk.key], w=[Va0.key])
        bt = self.bank()
        btb = bt.ap.bitcast(BF16)
        P.pe(lambda e_: e_.transpose(out=btb[:, 0:128], in_=k_b.ap, identity=identb.ap), r=[k_b.key, identb.key], w=[bt.key])
        P.act(lambda e_: e_.copy(out=kT0.ap, in_=btb[:, 0:128]), r=[bt.key], w=[kT0.key])

    def rotate(self, b0, b1, rot, n, dstT, tmpA, tmpB, scale_tab):
        P = self.P
        v3 = lambda ap: ap.rearrange("p (h t) -> p h t", t=n)
        cosb, sinb = bcast_mid(rot[:, 0, 0:n], 4), bcast_mid(rot[:, 1, 0:n], 4)
        A, B = v3(tmpA[:, 0:4 * n]), v3(tmpB[:, 0:4 * n])
        x0, x1 = v3(b0[:, 0:4 * n]), v3(b1[:, 0:4 * n])
        tt = lambda o, a, b, op, r, w: P.dve(lambda e_: e_.tensor_tensor(out=o, in0=a, in1=b, op=op), r=r, w=w)
        for par in range(2):
            if par == 0:
                tt(A, x0, cosb, ALU.mult, [b0.key, rot.key], [tmpA.key])
                tt(B, x1, sinb, ALU.mult, [b1.key, rot.key], [tmpB.key])
                op2 = ALU.subtract
            else:
                tt(A, x1, cosb, ALU.mult, [b1.key, rot.key], [tmpA.key])
                tt(B, x0, sinb, ALU.mult, [b0.key, rot.key], [tmpB.key])
                op2 = ALU.add
            if scale_tab is None:
                tt(dstT[:, :, par, 0:n], A, B, op2, [tmpA.key, tmpB.key], [dstT.key])
            else:
                tt(A, A, B, op2, [tmpA.key, tmpB.key], [tmpA.key])
                tt(dstT[:, :, par, 0:n], A, scale_tab.ap, ALU.mult, [tmpA.key, scale_tab.key], [dstT.key])

    def odd_A(self, o, layer):
        P, di, sc, U, W1 = self.P, self.di, self.sc, self.U, self.W1
        identb = self.identb
        U.reset()
        P.dma("sp", self.gbc.ap, di["ngain"].ap()[layer:layer + 1, :].partition_broadcast(128), w=[self.gbc.key])
        self.load_w(self.W2, 0, di["wo_out"].ap()[o], 0, D, 0, nkc=16)
        S = U.alloc("S", [128, 8, 512], F32)
        kdec = U.alloc("kdec", [128, 4, 128], F32)
        xt = U.rot("xt", [128, D], F32, 2)
        xnT = U.rot("xnT", [128, 8, 128], BF16, 2)
        rot = U.rot("rot", [128, 2, 128], F32, 2)
        tmpA = U.alloc("tmpA", [128, 512], F32)
        tmpB = U.alloc("tmpB", [128, 512], F32)
        kT = U.rot("kT", [128, 4, 2, 128], BF16, 2)
        kt = U.rot("kt", [128, D], BF16, 2)
        v = U.rot("v", [128, 2048], BF16, 2)
        self.oddA_S = S
        P.dma("sp", kdec.ap, di["kdec"].ap(), w=[kdec.key])
        P.dve(lambda e_: e_.memset(S.ap, 0.0), w=[(S.key, t) for t in range(8)])
        acut = self.cfg.get("acut", 0)

        def load(n):
            x = self.nxt(xt, "a_xt")
            P.dma("sp", x.ap, sc["xd"].ap()[n], r=[("xd", n)], w=[x.key])
            rt = self.nxt(rot, "a_rot")
            P.dma("sp", rt.ap, di["rotp"].ap()[:, :, n * 128:(n + 1) * 128].rearrange("c p t -> p c t"), w=[rt.key])
            return x, rt

        def chunk(n, x, rt):
            xT = self.nxt(xnT, "a_xnT")
            self.rmsnorm_T(x.ap, x.key, 128, xT)
            P.dma("sp", sc["xnT"].ap()[n], xT.ap.rearrange("p k t -> p (k t)"), r=[xT.key], w=[("sxnT", n)])
            if acut == 1:
                return
            b0, b1 = self.bank(), self.bank()
            for h in range(4):
                for par, bb in ((0, b0), (1, b1)):
                    self.proj_feat(xT, 128, W1, h * 256 + par * 128, bb, h * 128)
            k_T = self.nxt(kT, "a_kT")
            self.rotate(b0, b1, rt, 128, k_T, tmpA, tmpB, kdec)
            P.dma("sp", sc["kT"].ap()[n], k_T.ap.rearrange("p h c t -> p (h c t)"), r=[k_T.key], w=[("skT", n)])
            if acut == 2:
                return
            bt = self.bank()
            btb = bt.ap.bitcast(BF16)
            for t in range(8):
                P.pe(lambda e_, t=t, btb=btb, k_T=k_T: e_.transpose(out=btb[:, t * 128:(t + 1) * 128],
                                                                    in_=k_T[:, t // 2, t % 2, :], identity=identb.ap),
                     r=[k_T.key, identb.key], w=[bt.key])
            k_t = self.nxt(kt, "a_kt")
            for h in range(4):
                P.act(lambda e_, h=h, btb=btb, k_t=k_t: e_.activation(out=k_t[:, h * 256:(h + 1) * 256],
                                                                     in_=btb[:, h * 256:(h + 1) * 256], func=AF.Copy,
                                                                     scale=float(GAM[h] ** 128)),
                      r=[bt.key], w=[k_t.key])
            P.dma("sp", sc["kt"].ap()[n], k_t.ap, r=[k_t.key], w=[("skt", n)])
            if acut == 3:
                return
            vv = self.nxt(v, "a_v")
            for blk in range(4):
                bv = self.bank()
                self.proj_tok(xT, 128, W1, 1024 + blk * 512, 512, bv)
                P.act(lambda e_, blk=blk, bv=bv, vv=vv: e_.copy(out=vv[:, blk * 512:(blk + 1) * 512], in_=bv.ap),
                      r=[bv.key], w=[vv.key])
            P.dma("sp", sc["v"].ap()[n], vv.ap, r=[vv.key], w=[("sv", n)])
            if acut == 4:
                return
            self.state_update(S, None, k_t, vv)

        cur = load(0)
        for n in range(NCH):
            nx = load(n + 1) if n + 1 < NCH else None
            chunk(n, *cur)
            cur = nx
        if acut:
            P.barrier()
            return
        P.dma("sp", sc["cc_src"].ap().rearrange("(t p) e -> p t e", p=128), S.ap, r=[(S.key, t) for t in range(8)], w=["cc_src"])
        if self.cfg.get("nocc"):
            P.dma("sp", sc["cc_dst"].ap()[0:1024, :], sc["cc_src"].ap(), r=["cc_src"], w=["cc_dst"])
        else:
            P.op("pool", lambda e_: e_.collective_compute(
                "AllGather", ALU.bypass, replica_groups=[list(range(8))],
                ins=[sc["cc_src"].ap().opt()], outs=[sc["cc_dst"].ap().opt()]),
                reads=["cc_src"], writes=["cc_dst"], cc=True)
        if self.cfg.get("sample", True):
            self.sample_A(o, layer, tmpA, tmpB)
        wo = di["wo_in"].ap()[o]
        self.load_w(self.W1, 0, wo, 0, 1024, 0)
        self.load_w(self.W1, 0, wo, 4096, 2048, 1024)
        P.barrier()

    def state_update(self, S, Sbf, k_t, vv):
        P = self.P
        for h in range(4):
            for par in range(2):
                t = h * 2 + par
                bs_ = self.bank()
                P.pe(lambda e_, t=t, h=h, bs_=bs_: e_.matmul(out=bs_.ap, lhsT=k_t[:, t * 128:(t + 1) * 128],
                                                             rhs=vv[:, h * 512:(h + 1) * 512], start=True, stop=True),
                     r=[k_t.key, vv.key], w=[bs_.key])
                P.dve(lambda e_, t=t, h=h, bs_=bs_: e_.scalar_tensor_tensor(
                    out=S[:, t, :], in0=S[:, t, :], scalar=float(GAM[h] ** 128), in1=bs_.ap, op0=ALU.mult, op1=ALU.add),
                    r=[(S.key, t), bs_.key], w=[(S.key, t)])
                if Sbf is not None:
                    P.act(lambda e_, t=t: e_.copy(out=Sbf[:, t, :], in_=S[:, t, :]), r=[(S.key, t)], w=[(Sbf.key, t)])

    def odd_B1(self, o, layer):
        P, di, do, sc, U, W1 = self.P, self.di, self.do, self.sc, self.U, self.W1
        caus, epsc = self.caus, self.epsc
        U.reset()
        if o + 1 < 2:
            self.load_w(self.W3, 0, di["we_in"].ap()[o + 1], 0, E_IN, 0)
        S = U.alloc("S", [128, 8, 512], F32)
        Sbf = U.alloc("Sbf", [128, 8, 512], BF16)
        coef = U.alloc("coef", [128, 32], F32)
        epsp = U.alloc("epsp", [128, 4], F32)
        tmpA = U.alloc("tmpA", [128, 512], F32)
        tmpB = U.alloc("tmpB", [128, 512], F32)
        xnT = U.rot("xnT", [128, 8, 128], BF16, 2)
        rot = U.rot("rot", [128, 2, 128], F32, 2)
        kT = U.rot("kT", [128, 4, 2, 128], BF16, 2)
        kt = U.rot("kt", [128, D], BF16, 2)
        v = U.rot("v", [128, 2048], BF16, 2)
        stg = [Buf(b_.ap.bitcast(F32).rearrange("p (c e) -> p c e", e=512), b_.key) for b_ in v]
        qT = U.alloc("qT", [128, 4, 2, 128], BF16)
        attT = U.alloc("attT", [128, 4, 128], BF16)
        on = U.alloc("on", [128, 2048], BF16)
        tS = U.rot("tS", [128, 512], F32, 2)
        bst = U.alloc("bst", [128, 4, 6], F32)
        mv = U.alloc("mv", [128, 4, 2], F32)
        gst = U.alloc("gst", [128, 16], F32)
        P.dma("sp", coef.ap, di["coef"].ap(), w=[coef.key])
        P.dma("sp", epsp.ap, di["epsp"].ap(), w=[epsp.key])
        cd = sc["cc_dst"].ap()
        for h in range(4):
            for j in (0, 1, 2, 4, 5, 6):
                s_ = self.nxt(stg, "b_stg")
                src = cd[j * 1024 + h * 256:j * 1024 + (h + 1) * 256, :].rearrange("(c p) e -> p c e", p=128)
                P.dma("sp", s_.ap, src, r=["cc_dst"], w=[s_.key])
                keys = [(S.key, 2 * h), (S.key, 2 * h + 1)]
                if j == 0:
                    P.dve(lambda e_, h=h, j=j, s_=s_: e_.tensor_scalar(
                        out=S[:, 2 * h:2 * h + 2, :], in0=s_.ap, scalar1=coef[:, j * 4 + h:j * 4 + h + 1], scalar2=None,
                        op0=ALU.mult), r=[s_.key, coef.key], w=keys)
                else:
                    P.dve(lambda e_, h=h, j=j, s_=s_: e_.scalar_tensor_tensor(
                        out=S[:, 2 * h:2 * h + 2, :], in0=s_.ap, scalar=coef[:, j * 4 + h:j * 4 + h + 1],
                        in1=S[:, 2 * h:2 * h + 2, :], op0=ALU.mult, op1=ALU.add), r=[s_.key, coef.key] + keys, w=keys)
            P.act(lambda e_, h=h: e_.copy(out=Sbf[:, 2 * h:2 * h + 2, :], in_=S[:, 2 * h:2 * h + 2, :]),
                  r=[(S.key, 2 * h), (S.key, 2 * h + 1)], w=[(Sbf.key, 2 * h), (Sbf.key, 2 * h + 1)])
        def chunk(n):
            xT = self.nxt(xnT, "b_xnT")
            P.dma("sp", xT.ap.rearrange("p k t -> p (k t)"), sc["xnT"].ap()[n], r=[("sxnT", n)], w=[xT.key])
            k_T = self.nxt(kT, "b_kT")
            P.dma("sp", k_T.ap.rearrange("p h c t -> p (h c t)"), sc["kT"].ap()[n], r=[("skT", n)], w=[k_T.key])
            k_t = self.nxt(kt, "b_kt")
            P.dma("sp", k_t.ap, sc["kt"].ap()[n], r=[("skt", n)], w=[k_t.key])
            vv = self.nxt(v, "b_v")
            P.dma("sp", vv.ap, sc["v"].ap()[n], r=[("sv", n)], w=[vv.key])
            rt = self.nxt(rot, "b_rot")
            P.dma("sp", rt.ap, di["rotp"].ap()[:, :, n * 128:(n + 1) * 128].rearrange("c p t -> p c t"), w=[rt.key])
            b0, b1 = self.bank(), self.bank()
            for h in range(4):
                for par, bb in ((0, b0), (1, b1)):
                    self.proj_feat(xT, 128, W1, h * 256 + par * 128, bb, h * 128)
            self.rotate(b0, b1, rt, 128, qT, tmpA, tmpB, None)
            ba = self.bank()
            for h in range(4):
                for par in range(2):
                    P.pe(lambda e_, h=h, par=par, k_T=k_T: e_.matmul(out=ba[:, h * 128:(h + 1) * 128], lhsT=k_T[:, h, par, :],
                                                                     rhs=qT[:, h, par, :], start=(par == 0), stop=(par == 1)),
                         r=[k_T.key, qT.key], w=[ba.key])
            P.dve(lambda e_: e_.tensor_tensor(out=attT.ap, in0=ba.ap.rearrange("p (h t) -> p h t", t=128),
                                              in1=bcast_mid(caus.ap, 4), op=ALU.mult), r=[ba.key, caus.key], w=[attT.key])
            bo = []
            for h in range(4):
                b_ = self.bank()
                bo.append(b_)
                P.pe(lambda e_, h=h, b_=b_, vv=vv: e_.matmul(out=b_.ap, lhsT=attT[:, h, :], rhs=vv[:, h * 512:(h + 1) * 512],
                                                             start=True, stop=False), r=[attT.key, vv.key], w=[b_.key])
                for par in range(2):
                    P.pe(lambda e_, h=h, par=par, b_=b_: e_.matmul(out=b_.ap, lhsT=qT[:, h, par, :], rhs=Sbf[:, 2 * h + par, :],
                                                                   start=False, stop=(par == 1)),
                         r=[qT.key, (Sbf.key, 2 * h + par)], w=[b_.key])
                P.dve(lambda e_, h=h, b_=b_: e_.bn_stats(out=bst[:, h, :], in_=b_.ap), r=[b_.key], w=[(bst.key, h)])
                P.dve(lambda e_, h=h: e_.bn_aggr(out=mv[:, h, :], in_=bst[:, h, :]), r=[(bst.key, h)], w=[(mv.key, h)])
            mvk = [(mv.key, h) for h in range(4)]
            P.dve(lambda e_: e_.tensor_tensor(out=gst[:, 0:4], in0=mv[:, :, 1], in1=epsp.ap, op=ALU.add),
                  r=mvk + [epsp.key], w=[gst.key])
            P.act(lambda e_: e_.activation(out=gst[:, 4:8], in_=gst[:, 0:4], func=AF.Ln), r=[gst.key], w=[gst.key])
            P.act(lambda e_: e_.activation(out=gst[:, 8:12], in_=gst[:, 4:8], func=AF.Exp, scale=-0.5), r=[gst.key], w=[gst.key])
            P.dve(lambda e_: e_.scalar_tensor_tensor(out=gst[:, 12:16], in0=mv[:, :, 0], scalar=-1.0, in1=gst[:, 8:12],
                                                     op0=ALU.mult, op1=ALU.mult), r=mvk + [gst.key], w=[gst.key])
            for h in range(4):
                P.act(lambda e_, h=h: e_.activation(out=on[:, h * 512:(h + 1) * 512], in_=bo[h].ap, func=AF.Identity,
                                                    scale=gst[:, 8 + h:9 + h], bias=gst[:, 12 + h:13 + h]),
                      r=[bo[h].key, gst.key], w=[(on.key, h)])
            for h in range(4):
                bg = self.bank()
                self.proj_tok(xT, 128, W1, 1024 + h * 512, 512, bg)
                ts_ = self.nxt(tS, "b_tS")
                self.sigmoid_from(bg.ap, bg.key, ts_, 128)
                P.dve(lambda e_, bg=bg, ts_=ts_: e_.tensor_tensor(out=ts_.ap, in0=bg.ap, in1=ts_.ap, op=ALU.mult),
                      r=[bg.key, ts_.key], w=[ts_.key])
                P.dve(lambda e_, h=h, ts_=ts_: e_.tensor_tensor(out=on[:, h * 512:(h + 1) * 512], in0=on[:, h * 512:(h + 1) * 512],
                                                              in1=ts_.ap, op=ALU.mult),
                      r=[(on.key, h), ts_.key], w=[(on.key, h)])
            P.dma("sp", sc["mix"].ap()[n], on.ap, r=[(on.key, h) for h in range(4)], w=[("smix", n)])
            self.state_update(S, Sbf, k_t, vv)

        for n in range(NCH):
            chunk(n)
        P.dma("sp", do["retp"].ap()[o].rearrange("h (i two) e -> i h two e", two=2),
              S.ap.rearrange("p (h c) e -> p h c e", c=2), r=[(S.key, t) for t in range(8)])
        P.barrier()

    def odd_B2(self, o, layer, final):
        P, di, do, sc, U, W2 = self.P, self.di, self.do, self.sc, self.U, self.W2
        U.reset()
        xt = U.rot("xt", [128, D], F32, 2)
        mix = U.rot("mix", [128, 2048], BF16, 2)
        mixT = U.rot("mixT", [128, 16, 128], BF16, 2)
        yo = U.rot("yo", [128, D], F32, 2)
        if final:
            P.dma("sp", self.gbc.ap, di["ngain"].ap()[4:5, :].partition_broadcast(128), w=[self.gbc.key])
        def load(n):
            x = self.nxt(xt, "c_xt")
            P.dma("sp", x.ap, sc["xd"].ap()[n], r=[("xd", n)], w=[x.key])
            m = self.nxt(mix, "c_mix")
            P.dma("sp", m.ap, sc["mix"].ap()[n], r=[("smix", n)], w=[m.key])
            return x, m

        def chunk(n, x, m):
            mT = self.nxt(mixT, "c_mixT")
            self.out_proj(m, 16, mT, W2, x, 128)
            if final:
                self.final_norm(x, 128, self.nxt(yo, "c_yo"), do["yp"].ap()[n * 128:(n + 1) * 128, :])
            else:
                P.dma("sp", sc["xd"].ap()[n], x.ap, r=[x.key], w=[("xd", n)])

        cur = load(0)
        for n in range(NCH):
            nx = load(n + 1) if n + 1 < NCH else None
            chunk(n, *cur)
            cur = nx
        if final and self.cfg.get("sample", True):
            ys = U.alloc("ys", [64, D], F32)
            self.final_norm(self.xs, 64, ys, do["ys"].ap())
        if not final:
            self.load_w(self.W2, 0, di["we_out"].ap()[o + 1], 0, D, 0)
        P.barrier()

    def final_norm(self, x, n, y, dst):
        P, epsc, gbc = self.P, self.epsc, self.gbc
        st = self.nxt(self.stat, "stat")
        junk = self.junk
        P.act(lambda e: e.activation(out=junk[0:n, :], in_=x[0:n, :], func=AF.Square, accum_out=st[0:n, 0:1]),
              r=[x.key], w=[junk.key, st.key])
        P.act(lambda e: e.activation(out=st[0:n, 1:2], in_=st[0:n, 0:1], func=AF.Ln, scale=1.0 / D, bias=epsc[0:n, 0:1]),
              r=[st.key, epsc.key], w=[st.key])
        P.act(lambda e: e.activation(out=st[0:n, 2:3], in_=st[0:n, 1:2], func=AF.Exp, scale=-0.5), r=[st.key], w=[st.key])
        P.dve(lambda e: e.scalar_tensor_tensor(out=y[0:n, :], in0=x[0:n, :], scalar=st[0:n, 2:3], in1=gbc[0:n, :],
                                               op0=ALU.mult, op1=ALU.mult), r=[x.key, st.key, gbc.key], w=[y.key])
        P.dma("sp", dst, y[0:n, :], r=[y.key])

    def even_sample(self, e, layer):
        P, di, do, sc, U = self.P, self.di, self.do, self.sc, self.U
        W3, W2, identb, identf, xs = self.W3, self.W2, self.identb, self.identf, self.xs
        U.reset()
        c = self.even_consts(e, layer, False)
        esink, lng = c["esink"], c["lng"]
        selm = U.alloc("selm", [4, 64], F32)
        w4 = U.alloc("w4", [4, 4, 4], F32)
        bs4 = U.alloc("bs4", [4, 4], F32)
        Z = U.alloc("Z", [4, 4, 64], F32)
        bdm = U.alloc("bdm", [64, 64], F32)
        wmTs = U.alloc("wmTs", [64, 4, 64], BF16)
        bsbs = U.alloc("bsbs", [64, 4], F32)
        P.dma("sp", selm.ap, di["selm"].ap(), w=[selm.key])
        P.dma("sp", bdm.ap, di["bdm"].ap(), w=[bdm.key])
        for g in range(4):
            P.dma("sp", w4[:, g, :], di["gws"].ap()[e][g, 0:4, 0:4].rearrange("t a -> a t"), w=[w4.key],
                  allow_slow_non_contiguous=True)
        P.dma("sp", bs4.ap, di["gbs"].ap()[e][:, 0:4].rearrange("g t -> t g"), w=[bs4.key], allow_slow_non_contiguous=True)
        P.dve(lambda e_: e_.tensor_copy(out=Z.ap.rearrange("a g (b t) -> a g b t", t=4),
                                        in_=w4.ap.unsqueeze(2).to_broadcast([4, 4, 16, 4])), r=[w4.key], w=[Z.key])
        bk = self.bank()
        for g in range(4):
            P.pe(lambda e_, g=g: e_.matmul(out=bk[0:64, g * 64:(g + 1) * 64], lhsT=selm.ap, rhs=Z[:, g, :], start=True, stop=True),
                 r=[selm.key, Z.key], w=[bk.key])
        P.dve(lambda e_: e_.tensor_tensor(out=wmTs.ap, in0=bk[0:64, 0:256].rearrange("p (g c) -> p g c", c=64),
                                          in1=bcast_mid(bdm.ap, 4), op=ALU.mult), r=[bk.key, bdm.key], w=[wmTs.key])
        bk2 = self.bank()
        P.pe(lambda e_: e_.matmul(out=bk2[0:64, 0:4], lhsT=selm.ap, rhs=bs4.ap, start=True, stop=True),
             r=[selm.key, bs4.key], w=[bk2.key])
        P.act(lambda e_: e_.copy(out=bsbs.ap, in_=bk2[0:64, 0:4]), r=[bk2.key], w=[bsbs.key])
        bS = U.alloc("bS", [128, 8, 4], F32)
        bN = U.alloc("bN", [4, 8, 4], F32)
        xnT = U.alloc("xnT", [128, 8, 64], BF16)
        kvs = U.alloc("kvs", [64, 256], F32)
        kbs = U.alloc("kbs", [64, 128], BF16)
        kTs = U.alloc("kTs", [128, 64], BF16)
        qTs = U.alloc("qTs", [128, 4, 64], BF16)
        ckb = U.rot("ckb", [128, 128], BF16, 3)
        ckT = U.rot("ckT", [128, 128], BF16, 2)
        Vc = U.alloc("Vc", [128, 16, 2, 65], BF16)
        Vn = U.alloc("Vn", [4, 16, 2, 65], BF16)
        scS = U.alloc("scS", [128, 512], F32)
        expS = U.alloc("expS", [128, 512], BF16)
        scN = U.alloc("scN", [4, 512], F32)
        expN = U.alloc("expN", [4, 512], BF16)
        aT = U.alloc("aT", [65, 512], F32)
        sel65 = U.alloc("sel65", [65, 64], F32)
        attn = U.alloc("attn", [128, 512], F32)
        tA = U.rot("tA", [128, 512], F32, 2)
        tB = U.rot("tB", [128, 512], F32, 2)
        bst = U.alloc("bst", [128, 6], F32)
        mv = U.alloc("mv", [128, 4], F32)
        vnb = U.alloc("vnb", [128, 512], BF16)
        gvt = U.alloc("gvt", [128, 512], F32)
        mix = U.alloc("mix", [128, D], BF16)
        mixT = U.alloc("mixT", [128, 8, 128], BF16)
        P.dma("sp", bS.ap, self.toeplitz(256, 128, 4), r=["G"], w=[bS.key])
        P.dma("sp", bN.ap, self.toeplitz(128, 4, 4), r=["G"], w=[bN.key])
        P.dve(lambda e_: e_.memset(Vc[:, :, :, 64:65], 1.0), w=[(Vc.key, "one")])
        P.dve(lambda e_: e_.memset(Vn[:, :, :, 64:65], 1.0), w=[Vn.key])
        P.dve(lambda e_: e_.memset(sel65.ap, 0.0), w=[sel65.key])
        P.dve(lambda e_: e_.memset(sel65[64:65, :], 1.0), r=[sel65.key], w=[sel65.key])
        esr = U.alloc("esr", [65, 512], F32)
        sk = U.alloc("sk", [65, 8], F32)
        P.dma("sp", sk[64:65, :], di["sinks"].ap()[e:e + 1, :], w=[sk.key])
        P.act(lambda e_: e_.activation(out=sk[64:65, :], in_=sk[64:65, :], func=AF.Exp), r=[sk.key], w=[sk.key])
        P.dve(lambda e_: e_.tensor_copy(out=esr[64:65, :].rearrange("p (h c) -> p h c", c=64), in_=bcast_last(sk[64:65, :], 64)),
              r=[sk.key], w=[esr.key])
        P.dma("sp", do["ksn"].ap()[e][:, 0:124, :], di["ck"].ap()[e][:, 4:128, :], w=[("ksn", e, 0)])
        P.dma("sp", do["vsn"].ap()[e][:, 0:124, :], di["cv"].ap()[e][:, 4:128, :], w=[("vsn", e, 0)])
        self.rmsnorm_T(xs[0:64, :], xs.key, 64, xnT)
        bkv = self.bank()
        self.proj_tok(xnT, 64, W3, 512, 256, bkv)
        P.act(lambda e_: e_.copy(out=kvs.ap, in_=bkv[0:64, 0:256]), r=[bkv.key], w=[kvs.key])
        P.dve(lambda e_: e_.tensor_copy(out=kbs.ap, in_=bkv[0:64, 0:128]), r=[bkv.key], w=[kbs.key])
        P.dma("sp", sc["kv4"].ap(), kvs.ap, r=[kvs.key], w=["kv4"])
        kv4 = sc["kv4"].ap().rearrange("(b t) c -> b t c", t=4)
        P.dma("sp", do["ksn"].ap()[e][:, 124:128, :], kv4[:, :, 0:128], r=["kv4"], w=[("ksn", e, 1)])
        P.dma("sp", do["vsn"].ap()[e][:, 124:128, :], kv4[:, :, 128:256], r=["kv4"], w=[("vsn", e, 1)])
        for g in range(2):
            P.dma("pool", Vn[0:4, :, g, 0:64], kv4[:, :, 128 + g * 64:192 + g * 64].rearrange("b t d -> t b d"),
                  r=["kv4", Vn.key], w=[Vn.key])
        bt = self.bank()
        btb = bt.ap.bitcast(BF16)
        P.pe(lambda e_: e_.transpose(out=btb[:, 0:64], in_=kbs.ap, identity=identb[0:64, 0:64]), r=[kbs.key, identb.key], w=[bt.key])
        P.act(lambda e_: e_.copy(out=kTs.ap, in_=btb[:, 0:64]), r=[bt.key], w=[kTs.key])
        bq = self.bank()
        for j in range(4):
            self.proj_feat(xnT, 64, W3, j * 128, bq, j * 64)
        P.act(lambda e_: e_.activation(out=qTs.ap.rearrange("p j t -> p (j t)"), in_=bq[:, 0:256], func=AF.Copy, scale=0.125),
              r=[bq.key], w=[qTs.key])
        SC, SN = self.reserve_bank(), self.reserve_bank()
        for b in range(16):
            ck_ = self.nxt(ckb, "s_ckb")
            P.dma("pool", ck_.ap, di["ck"].ap()[e, b], w=[ck_.key])
            P.dma("pool", Vc[:, b, :, 0:64], di["cv"].ap()[e, b].rearrange("s (g d) -> s g d", d=64), w=[(Vc.key, b)])
            bt2 = self.bank()
            bt2b = bt2.ap.bitcast(BF16)
            P.pe(lambda e_, ck_=ck_, bt2b=bt2b: e_.transpose(out=bt2b[:, 0:128], in_=ck_.ap, identity=identb.ap),
                 r=[ck_.key, identb.key], w=[bt2.key])
            cT = self.nxt(ckT, "s_ckT")
            P.act(lambda e_, cT=cT, bt2b=bt2b: e_.copy(out=cT.ap, in_=bt2b[:, 0:128]), r=[bt2.key], w=[cT.key])
            for g2 in range(2):
                rhs = qTs[g2 * 64:(g2 + 1) * 64, :, b * 4:(b + 1) * 4]
                first = (b == 0 and g2 == 0)
                c0 = b * 32 + g2 * 16
                P.pe(lambda e_, g2=g2, c0=c0, cT=cT, rhs=rhs, first=first: e_.matmul(
                    out=SC[:, c0:c0 + 16], lhsT=cT[g2 * 64:(g2 + 1) * 64, :], rhs=rhs, start=first, stop=True,
                    skip_group_check=True), r=[cT.key, qTs.key], w=[SC.key])
                P.pe(lambda e_, g2=g2, b=b, c0=c0, rhs=rhs, first=first: e_.matmul(
                    out=SN[0:4, c0:c0 + 16], lhsT=kTs[g2 * 64:(g2 + 1) * 64, b * 4:(b + 1) * 4], rhs=rhs,
                    start=first, stop=True, skip_group_check=True), r=[kTs.key, qTs.key], w=[SN.key])
        P.dve(lambda e_: e_.tensor_tensor(out=scS.ap.rearrange("p (b c) -> p b c", c=32), in0=SC.ap.rearrange("p (b c) -> p b c", c=32),
                                          in1=bcast_mid(bS.ap.rearrange("p h t -> p (h t)"), 16), op=ALU.add),
              r=[SC.key, bS.key], w=[scS.key])
        P.act(lambda e_: e_.activation(out=expS.ap, in_=scS.ap, func=AF.Exp), r=[scS.key], w=[expS.key])
        P.dve(lambda e_: e_.tensor_tensor(out=scN[0:4, :].rearrange("p (b c) -> p b c", c=32), in0=SN[0:4, :].rearrange("p (b c) -> p b c", c=32),
                                          in1=bcast_mid(bN.ap.rearrange("p h t -> p (h t)"), 16), op=ALU.add),
              r=[SN.key, bN.key], w=[scN.key])
        P.act(lambda e_: e_.activation(out=expN[0:4, :], in_=scN[0:4, :], func=AF.Exp), r=[scN.key], w=[expN.key])
        self.release_banks()
        PT = self.bank()
        for b in range(16):
            for g2 in range(2):
                first = (b == 0 and g2 == 0)
                c0 = b * 32 + g2 * 16
                P.pe(lambda e_, b=b, g2=g2, c0=c0, first=first: e_.matmul(
                    out=PT[0:65, c0:c0 + 16], lhsT=Vc[:, b, g2, :], rhs=expS[:, c0:c0 + 16], start=first, stop=False,
                    skip_group_check=True), r=[(Vc.key, b), (Vc.key, "one"), expS.key], w=[PT.key])
                P.pe(lambda e_, b=b, g2=g2, c0=c0: e_.matmul(
                    out=PT[0:65, c0:c0 + 16], lhsT=Vn[0:4, b, g2, :], rhs=expN[0:4, c0:c0 + 16], start=False, stop=True,
                    skip_group_check=True), r=[Vn.key, expN.key], w=[PT.key])
        P.act(lambda e_: e_.copy(out=aT.ap.rearrange("p (h b t) -> p h b t", b=16, t=4),
                                 in_=PT[0:65, :].rearrange("p (b h t) -> p h b t", h=8, t=4)), r=[PT.key], w=[aT.key])
        P.dve(lambda e_: e_.tensor_tensor(out=aT[64:65, :], in0=aT[64:65, :], in1=esr[64:65, :], op=ALU.add),
              r=[aT.key, esr.key], w=[aT.key])
        P.dve(lambda e_: e_.reciprocal(out=aT[64:65, :], in_=aT[64:65, :]), r=[aT.key], w=[aT.key])
        BC = self.bank()
        P.pe(lambda e_: e_.matmul(out=BC[0:64, :], lhsT=sel65.ap, rhs=aT.ap, start=True, stop=True), r=[sel65.key, aT.key], w=[BC.key])
        P.dve(lambda e_: e_.tensor_tensor(out=aT[0:64, :], in0=aT[0:64, :], in1=BC[0:64, :], op=ALU.mult),
              r=[aT.key, BC.key], w=[aT.key])
        TB = self.bank()
        for h in range(8):
            P.pe(lambda e_, h=h: e_.transpose(out=TB[0:64, h * 64:(h + 1) * 64], in_=aT[0:64, h * 64:(h + 1) * 64],
                                              identity=identf[0:64, 0:64]), r=[aT.key, identf.key], w=[TB.key])
        P.act(lambda e_: e_.copy(out=attn[0:64, :], in_=TB[0:64, :]), r=[TB.key], w=[attn.key])
        self.even_gates(xnT, 64, attn, lng, wmTs, bsbs, tA, tB, bst, mv, vnb, mix, (gvt, do["gv"].ap()[e]))
        self.out_proj(mix, 8, mixT, W2, xs, 64)
        P.barrier()

    def sample_A(self, o, layer, tmpA, tmpB):
        P, di, sc, U, W1, identb, xs = self.P, self.di, self.sc, self.U, self.W1, self.identb, self.xs
        xnT = U.alloc("s_xnT", [128, 8, 64], BF16)
        rots = U.alloc("s_rot", [128, 2, 64], F32)
        kdecs = U.alloc("s_kdec", [128, 4, 64], F32)
        kTs = U.alloc("s_kT", [128, 4, 2, 64], BF16)
        kts = U.alloc("s_kt", [64, D], BF16)
        vs = U.alloc("s_v", [64, 2048], BF16)
        P.dma("sp", rots.ap, di["rots"].ap().rearrange("c p t -> p c t"), w=[rots.key])
        P.dma("sp", kdecs.ap, di["kdecs"].ap(), w=[kdecs.key])
        self.rmsnorm_T(xs[0:64, :], xs.key, 64, xnT)
        P.dma("sp", sc["s_xnT"].ap(), xnT.ap.rearrange("p k t -> p (k t)"), r=[xnT.key], w=["s_xnT"])
        b0, b1 = self.bank(), self.bank()
        for h in range(4):
            for par, bb in ((0, b0), (1, b1)):
                self.proj_feat(xnT, 64, W1, h * 256 + par * 128, bb, h * 64)
        self.rotate(b0, b1, rots, 64, kTs, tmpA, tmpB, kdecs)
        P.dma("sp", sc["s_kT"].ap(), kTs.ap.rearrange("p h c t -> p (h c t)"), r=[kTs.key], w=["s_kT"])
        bt = self.bank()
        btb = bt.ap.bitcast(BF16)
        for t in range(8):
            P.pe(lambda e_, t=t: e_.transpose(out=btb[0:64, t * 128:(t + 1) * 128], in_=kTs[:, t // 2, t % 2, :], identity=identb.ap),
                 r=[kTs.key, identb.key], w=[bt.key])
        for h in range(4):
            P.act(lambda e_, h=h: e_.activation(out=kts[:, h * 256:(h + 1) * 256], in_=btb[0:64, h * 256:(h + 1) * 256],
                                                func=AF.Copy, scale=float(GAM[h] ** 4)), r=[bt.key], w=[kts.key])
        P.dma("sp", sc["s_kt"].ap(), kts.ap, r=[kts.key], w=["s_kt"])
        for blk in range(4):
            bv = self.bank()
            self.proj_tok(xnT, 64, W1, 1024 + blk * 512, 512, bv)
            P.act(lambda e_, blk=blk, bv=bv: e_.copy(out=vs[:, blk * 512:(blk + 1) * 512], in_=bv[0:64, :]), r=[bv.key], w=[vs.key])
        P.dma("sp", sc["s_v"].ap(), vs.ap, r=[vs.key], w=["s_v"])

    def sample_B(self, o, layer):
        P, di, do, sc, U, W1, W2 = self.P, self.di, self.do, self.sc, self.U, self.W1, self.W2
        identf, xs, epsc = self.identf, self.xs, self.epsc
        U.reset()
        xnT = U.alloc("xnT", [128, 8, 64], BF16)
        rots = U.alloc("rot", [128, 2, 64], F32)
        kTs = U.alloc("kT", [128, 4, 2, 64], BF16)
        kts = U.alloc("kt", [64, D], BF16)
        ktm = U.rot("ktm", [64, D], BF16, 2)
        vs = U.alloc("v", [64, 2048], BF16)
        qTs = U.alloc("qT", [128, 4, 2, 64], BF16)
        tmpA = U.alloc("tmpA", [128, 512], F32)
        tmpB = U.alloc("tmpB", [128, 512], F32)
        bdm = U.alloc("bdm", [64, 64], F32)
        rowm = U.alloc("rowm", [64, 16], F32)
        epsps = U.alloc("epsps", [64, 4], F32)
        attT = U.alloc("attT", [64, 4, 64], BF16)
        sg = U.alloc("sg", [64, 2048], BF16)
        tS = U.rot("tS", [64, 512], F32, 2)
        S0 = U.rot("S0", [128, 2, 512], F32, 3)
        S0b = U.rot("S0b", [128, 2, 512], BF16, 2)
        So = U.rot("So", [128, 2, 512], F32, 2)
        oT = U.alloc("oT", [128, 16, 64], F32)
        on = U.alloc("on", [64, 2048], BF16)
        bst = U.alloc("bst", [64, 4, 6], F32)
        mv = U.alloc("mv", [64, 4, 2], F32)
        gst = U.alloc("gst", [64, 16], F32)
        mixT = U.alloc("mixT", [128, 16, 128], BF16)
        P.dma("sp", xnT.ap.rearrange("p k t -> p (k t)"), sc["s_xnT"].ap(), r=["s_xnT"], w=[xnT.key])
        P.dma("sp", kTs.ap.rearrange("p h c t -> p (h c t)"), sc["s_kT"].ap(), r=["s_kT"], w=[kTs.key])
        P.dma("sp", kts.ap, sc["s_kt"].ap(), r=["s_kt"], w=[kts.key])
        P.dma("sp", vs.ap, sc["s_v"].ap(), r=["s_v"], w=[vs.key])
        P.dma("sp", rots.ap, di["rots"].ap().rearrange("c p t -> p c t"), w=[rots.key])
        P.dma("sp", bdm.ap, di["bdm"].ap(), w=[bdm.key])
        P.dma("sp", rowm.ap, di["rowm"].ap(), w=[rowm.key])
        P.dma("sp", epsps.ap, di["epsps"].ap(), w=[epsps.key])
        b0, b1 = self.bank(), self.bank()
        for h in range(4):
            for par, bb in ((0, b0), (1, b1)):
                self.proj_feat(xnT, 64, W1, h * 256 + par * 128, bb, h * 64)
        self.rotate(b0, b1, rots, 64, qTs, tmpA, tmpB, None)
        for blk in range(4):
            bg = self.bank()
            self.proj_tok(xnT, 64, W1, 1024 + blk * 512, 512, bg)
            ts_ = self.nxt(tS, "sb_tS")
            self.sigmoid_from(bg[0:64, :], bg.key, ts_, 64)
            P.dve(lambda e_, blk=blk, bg=bg, ts_=ts_: e_.tensor_tensor(out=sg[:, blk * 512:(blk + 1) * 512], in0=bg[0:64, :],
                                                                       in1=ts_.ap, op=ALU.mult), r=[bg.key, ts_.key], w=[sg.key])
        ba = self.bank()
        for h in range(4):
            for par in range(2):
                P.pe(lambda e_, h=h, par=par: e_.matmul(out=ba[0:64, h * 64:(h + 1) * 64], lhsT=kTs[:, h, par, :], rhs=qTs[:, h, par, :],
                                                        start=(par == 0), stop=(par == 1)), r=[kTs.key, qTs.key], w=[ba.key])
        P.dve(lambda e_: e_.tensor_tensor(out=attT.ap, in0=ba[0:64, 0:256].rearrange("p (h t) -> p h t", t=64),
                                          in1=bcast_mid(bdm.ap, 4), op=ALU.mult), r=[ba.key, bdm.key], w=[attT.key])
        OT = [self.reserve_bank(), self.reserve_bank()]
        for h in range(4):
            for ec in range(4):
                t = h * 4 + ec
                P.pe(lambda e_, h=h, ec=ec, t=t: e_.matmul(out=OT[t // 8][:, (t % 8) * 64:(t % 8 + 1) * 64],
                                                           lhsT=vs[:, h * 512 + ec * 128:h * 512 + (ec + 1) * 128], rhs=attT[:, h, :],
                                                           start=(t % 8 == 0), stop=False, skip_group_check=True),
                     r=[vs.key, attT.key], w=[OT[t // 8].key])
        st = di["st"].ap()[o]
        rets = do["rets"].ap()[o]
        for b in range(16):
            km = self.nxt(ktm, "sb_ktm")
            P.dve(lambda e_, b=b, km=km: e_.tensor_scalar(out=km.ap, in0=kts.ap, scalar1=rowm[:, b:b + 1], scalar2=None, op0=ALU.mult),
                  r=[kts.key, rowm.key], w=[km.key])
            for h in range(4):
                s0 = self.nxt(S0, "sb_S0")
                P.dma("sp", s0.ap, st[b, h].rearrange("(i two) e -> i two e", two=2), w=[s0.key])
                s0b = self.nxt(S0b, "sb_S0b")
                P.pool(lambda e_, s0=s0, s0b=s0b: e_.tensor_copy(out=s0b.ap, in_=s0.ap), r=[s0.key], w=[s0b.key])
                for ec in range(4):
                    t = h * 4 + ec
                    for par in range(2):
                        P.pe(lambda e_, b=b, h=h, ec=ec, par=par, t=t, s0b=s0b: e_.matmul(
                            out=OT[t // 8][:, (t % 8) * 64 + b * 4:(t % 8) * 64 + (b + 1) * 4], lhsT=s0b[:, par, ec * 128:(ec + 1) * 128],
                            rhs=qTs[:, h, par, b * 4:(b + 1) * 4], start=False, stop=True, skip_group_check=True),
                            r=[s0b.key, qTs.key], w=[OT[t // 8].key])
                so = self.nxt(So, "sb_So")
                for par in range(2):
                    bs_ = self.bank()
                    tt = h * 2 + par
                    P.pe(lambda e_, tt=tt, h=h, km=km, bs_=bs_: e_.matmul(out=bs_.ap, lhsT=km[:, tt * 128:(tt + 1) * 128],
                                                                         rhs=vs[:, h * 512:(h + 1) * 512], start=True, stop=True),
                         r=[km.key, vs.key], w=[bs_.key])
                    P.dve(lambda e_, par=par, h=h, s0=s0, so=so, bs_=bs_: e_.scalar_tensor_tensor(
                        out=so[:, par, :], in0=s0[:, par, :], scalar=float(GAM[h] ** 4), in1=bs_.ap, op0=ALU.mult, op1=ALU.add),
                        r=[s0.key, bs_.key], w=[so.key])
                P.dma("sp", rets[b, h].rearrange("(i two) e -> i two e", two=2), so.ap, r=[so.key])
        for i in range(2):
            P.act(lambda e_, i=i: e_.copy(out=oT[:, i * 8:(i + 1) * 8, :], in_=OT[i].ap.rearrange("p (t c) -> p t c", c=64)),
                  r=[OT[i].key], w=[(oT.key, i)])
        self.release_banks()
        bo = []
        for h in range(4):
            b_ = self.bank()
            bo.append(b_)
            for ec in range(4):
                P.pe(lambda e_, h=h, ec=ec, b_=b_: e_.transpose(out=b_[0:64, ec * 128:(ec + 1) * 128], in_=oT[:, h * 4 + ec, :],
                                                                identity=identf.ap), r=[(oT.key, h // 2), identf.key], w=[b_.key])
            P.dve(lambda e_, h=h, b_=b_: e_.bn_stats(out=bst[:, h, :], in_=b_[0:64, :]), r=[b_.key], w=[(bst.key, h)])
            P.dve(lambda e_, h=h: e_.bn_aggr(out=mv[:, h, :], in_=bst[:, h, :]), r=[(bst.key, h)], w=[(mv.key, h)])
        mvk = [(mv.key, h) for h in range(4)]
        P.dve(lambda e_: e_.tensor_tensor(out=gst[:, 0:4], in0=mv[:, :, 1], in1=epsps.ap, op=ALU.add), r=mvk + [epsps.key], w=[gst.key])
        P.act(lambda e_: e_.activation(out=gst[:, 4:8], in_=gst[:, 0:4], func=AF.Ln), r=[gst.key], w=[gst.key])
        P.act(lambda e_: e_.activation(out=gst[:, 8:12], in_=gst[:, 4:8], func=AF.Exp, scale=-0.5), r=[gst.key], w=[gst.key])
        P.dve(lambda e_: e_.scalar_tensor_tensor(out=gst[:, 12:16], in0=mv[:, :, 0], scalar=-1.0, in1=gst[:, 8:12],
                                                 op0=ALU.mult, op1=ALU.mult), r=mvk + [gst.key], w=[gst.key])
        for h in range(4):
            P.act(lambda e_, h=h: e_.activation(out=on[:, h * 512:(h + 1) * 512], in_=bo[h][0:64, :], func=AF.Identity,
                                                scale=gst[:, 8 + h:9 + h], bias=gst[:, 12 + h:13 + h]),
                  r=[bo[h].key, gst.key], w=[(on.key, h)])
            P.dve(lambda e_, h=h: e_.tensor_tensor(out=on[:, h * 512:(h + 1) * 512], in0=on[:, h * 512:(h + 1) * 512],
                                                   in1=sg[:, h * 512:(h + 1) * 512], op=ALU.mult), r=[(on.key, h), sg.key], w=[(on.key, h)])
        onb = Buf(on.ap, on.key)
        self.out_proj_keys = [(on.key, h) for h in range(4)]
        self.out_proj(onb, 16, mixT, W2, xs, 64, mixkeys=[(on.key, h) for h in range(4)])
        P.barrier()

    def reserve_bank(self):
        b = self.bank()
        self.reserved.add(b.key)
        return b

    def release_banks(self):
        self.reserved.clear()

    def build(self):
        cfg = self.cfg
        self.declare()
        self.alloc_static()
        self.setup()
        di = self.di
        self.load_w(self.W3, 0, di["we_in"].ap()[0], 0, E_IN, 0)
        self.load_w(self.W2, 0, di["we_out"].ap()[0], 0, D, 0)
        depth = cfg.get("depth", 4)
        sample = cfg.get("sample", True)
        if cfg.get("cut") == 1:
            depth = 0
        for p in range(2):
            if depth > 2 * p:
                self.even_prompt(p, 2 * p)
                if sample:
                    self.even_sample(p, 2 * p)
            if depth > 2 * p + 1:
                self.odd_A(p, 2 * p + 1)
                if cfg.get("oddstop") == "A":
                    break
                self.odd_B1(p, 2 * p + 1)
                if cfg.get("oddstop") == "B1":
                    break
                if sample:
                    self.sample_B(p, 2 * p + 1)
                self.odd_B2(p, 2 * p + 1, final=(p == 1))
        if depth < 4:
            self.debug_tail()
        self.P.emit(self.stack)
        return self.nc

    def debug_tail(self):
        P, U, sc, do = self.P, self.U, self.sc, self.do
        U.reset()
        xt = U.rot("xt", [128, D], F32, 2)
        for n in range(NCH):
            x = self.nxt(xt, "d_xt")
            P.dma("sp", x.ap, sc["xd"].ap()[n], r=[("xd", n)], w=[x.key])
            P.dma("sp", do["yp"].ap()[n * 128:(n + 1) * 128, :], x.ap, r=[x.key])
        P.dma("sp", do["ys"].ap(), self.xs[0:64, :], r=[self.xs.key])


def _host_constants():
    c = {}
    c["ident"] = np.eye(128, dtype=np.float32)
    oh = np.zeros((33, 384), np.float32)
    for m in range(384):
        dist = m - 128
        if 0 <= dist < 128:
            if dist < 16:
                b = dist
            else:
                nf = np.float32(max(dist, 1))
                val = np.log(nf / np.float32(16)) / np.float32(math.log(128 / 16)) * np.float32(16)
                b = min(16 + int(np.float32(val)), 31)
            oh[b, m] = 1.0
        else:
            oh[32, m] = 1.0
    c["oh"] = oh
    j = np.arange(128, dtype=np.float64)
    kdec = np.stack([GAM[h] ** (-(j + 1.0)) / 16.0 for h in range(4)], 0)
    c["kdec"] = np.broadcast_to(kdec[None], (128, 4, 128)).astype(np.float32).copy()
    t4 = np.tile(np.arange(4, dtype=np.float64), 16)
    kdecs = np.stack([GAM[h] ** (-(t4 + 1.0)) / 16.0 for h in range(4)], 0)
    c["kdecs"] = np.broadcast_to(kdecs[None], (128, 4, 64)).astype(np.float32).copy()
    c["epsp"] = np.stack([EPS * GAM[h] ** (-2.0 * (j + 1.0)) for h in range(4)], 1).astype(np.float32)
    c["epsps"] = np.stack([EPS * GAM[h] ** (-2.0 * (t4 + 1.0)) for h in range(4)], 1).astype(np.float32)
    c["caus"] = (j[:, None] <= j[None, :]).astype(np.float32)
    bi = np.arange(64) // 4
    ti = np.arange(64) % 4
    c["bdm"] = ((bi[:, None] == bi[None, :]) & (ti[:, None] <= ti[None, :])).astype(np.float32)
    c["rowm"] = (bi[:, None] == np.arange(16)[None, :]).astype(np.float32)
    c["selm"] = (np.arange(4)[:, None] == ti[None, :]).astype(np.float32)
    ang = (1.0 / (10000.0 ** np.linspace(0.0, 1.0, 128, dtype=np.float32))).astype(np.float32)
    c["_ang"] = ang
    return c


def _rot_table(pos, ang):
    a = pos.astype(np.float32)[None, :] * ang[:, None]
    return np.stack([np.cos(a.astype(np.float64)), np.sin(a.astype(np.float64))], 0).astype(np.float32)


_CACHE = {}


def _get_nc(cfg_key, cfg):
    if cfg_key not in _CACHE:
        _CACHE[cfg_key] = Builder(cfg).build()
    return _CACHE[cfg_key]


def kernel(x_prompt, x_sample, cache_swa_k, cache_swa_v, state_ret, norm_gain, final_norm_gain,
           rel_bias_table, even_w_in, even_w_out, swa_sinks, gmlp_ws, gmlp_bs, gmlp_ln_gain,
           odd_w_in, odd_w_out, _cfg=None):
    cfg = dict(_cfg or {})
    f = lambda a: np.ascontiguousarray(np.asarray(a, dtype=np.float32))
    x_prompt, x_sample = f(x_prompt), f(x_sample)
    cache_swa_k, cache_swa_v, state_ret = f(cache_swa_k), f(cache_swa_v), f(state_ret)
    even_w_in, even_w_out, odd_w_in, odd_w_out = f(even_w_in), f(even_w_out), f(odd_w_in), f(odd_w_out)
    qperm = np.concatenate([np.r_[j * 64:(j + 1) * 64, (4 + j) * 64:(5 + j) * 64] for j in range(4)])
    eperm = np.concatenate([qperm, np.arange(512, E_IN)])
    we_in = np.ascontiguousarray(even_w_in[:, :, eperm])
    dei = np.concatenate([np.r_[h * 256:(h + 1) * 256:2, h * 256 + 1:(h + 1) * 256:2] for h in range(4)])
    operm = np.concatenate([dei, 1024 + dei, np.arange(2048, O_IN)])
    wo_in = np.ascontiguousarray(odd_w_in[:, :, operm])
    ngain = np.ascontiguousarray(np.concatenate([f(norm_gain), f(final_norm_gain)[None]], 0))
    hc = _host_constants()
    ang = hc.pop("_ang")
    shared = dict(we_in=we_in, we_out=even_w_out, wo_in=wo_in, wo_out=odd_w_out, ngain=ngain,
                  table=f(rel_bias_table), sinks=f(swa_sinks), gws=f(gmlp_ws), gbs=f(gmlp_bs), glg=f(gmlp_ln_gain), **hc)
    shared["rots"] = _rot_table(8192 + np.tile(np.arange(4), 16), ang)
    in_maps = []
    for c in range(8):
        b, r = c // 4, c % 4
        m = dict(shared)
        m["xp"] = np.ascontiguousarray(x_prompt[b, r * 2048:(r + 1) * 2048])
        m["xh"] = np.ascontiguousarray(x_prompt[b, r * 2048 - 128:r * 2048]) if r > 0 else np.zeros((128, D), np.float32)
        m["xs"] = np.ascontiguousarray(x_sample[16 * c:16 * c + 16].reshape(64, D))
        m["ck"] = np.ascontiguousarray(cache_swa_k[:, 16 * c:16 * c + 16].reshape(2, 16, 128, 128))
        m["cv"] = np.ascontiguousarray(cache_swa_v[:, 16 * c:16 * c + 16].reshape(2, 16, 128, 128))
        m["st"] = np.ascontiguousarray(state_ret[:, 16 * c:16 * c + 16]) if cfg.get("sample", True) else np.zeros((1, 1, 1, 2, 512), np.float32)
        m["rotp"] = _rot_table(r * 2048 + np.arange(2048), ang)
        coef = np.zeros((128, 32), np.float32)
        for j in range(3):
            if j < r:
                for h in range(4):
                    coef[:, (b * 4 + j) * 4 + h] = GAM[h] ** (2048.0 * (r - 1 - j))
        m["coef"] = coef
        hsel = np.zeros((128, 8), np.float32)
        if r > 0:
            hsel[:, b * 4 + r - 1] = 1.0
        m["hsel"] = hsel
        m["hneg"] = np.full((128, 1), NEG if r == 0 else 0.0, np.float32)
        in_maps.append(m)
    nc = _get_nc(tuple(sorted(cfg.items())), cfg)
    res = run_bass_kernel_spmd(nc, in_maps, core_ids=list(range(8)))
    R = res.results
    y_prompt = np.stack([np.concatenate([R[b * 4 + r]["yp"] for r in range(4)], 0) for b in range(2)], 0)
    y_sample = np.concatenate([R[c]["ys"].reshape(16, 4, D) for c in range(8)], 0)
    kvp = np.stack([R[b * 4 + 3]["kvp"] for b in range(2)], 1)
    new_k_p = kvp[..., 0:128].reshape(2, 2, 128, 2, 64).copy()
    new_v_p = kvp[..., 128:256].reshape(2, 2, 128, 2, 64).copy()
    new_k_s = np.concatenate([R[c]["ksn"] for c in range(8)], 1).reshape(2, 128, 128, 2, 64)
    new_v_s = np.concatenate([R[c]["vsn"] for c in range(8)], 1).reshape(2, 128, 128, 2, 64)
    new_ret_p = np.stack([R[b * 4 + 3]["retp"] for b in range(2)], 1)
    new_ret_s = np.concatenate([R[c]["rets"] for c in range(8)], 1) if cfg.get("sample", True) else np.zeros((2, 128, 4, 256, 512), np.float32)
    gv = np.concatenate([R[c]["gv"].reshape(2, 16, 4, 512) for c in range(8)], 1)
    outs = (y_prompt, y_sample, new_k_p, new_v_p, new_k_s, new_v_s, new_ret_p, new_ret_s, gv)
    return tuple(np.ascontiguousarray(o, dtype=np.float32) for o in outs)
```
